# Optimizing a Trainium2 kernel written in Bass

```python
import math
import jax, jax.numpy as jnp
from jax import lax
import numpy as np

D_MODEL = 1024
BATCH = 8
SEQ = 2048
DEPTH = 1
DEC_BATCH = 128
DEC_SEQ = 1
PAST_LEN = 16384
PAGE_SIZE = 128

D_MIX = 2 * D_MODEL
D_S5 = D_MIX // 2
S5_CH = 16
S5_GROUPS = D_S5 // S5_CH
S5_STATE = 64
D_SSD = D_MIX - D_S5
SSD_HEAD_DIM = 64
SSD_HEADS = D_SSD // SSD_HEAD_DIM
SSD_GROUPS = 2
SSD_HPG = SSD_HEADS // SSD_GROUPS
SSD_STATE = 128
SSD_CONV = 4
SSD_CHUNK = 128
D_CONV = D_SSD + 2 * SSD_GROUPS * SSD_STATE
D_IN_PROJ = D_S5 + D_SSD + D_CONV + SSD_HEADS
D_FF = 4 * D_MODEL
EPS = 1e-5

kernel_name = "hymba_s5_ssd_decoder_step"


def rmsnorm(x, w):
    xf = x.astype(jnp.float32)
    xf = xf * lax.rsqrt(jnp.mean(xf * xf, axis=-1, keepdims=True) + EPS)
    return (xf * w.astype(jnp.float32)).astype(x.dtype)


def _complex_affine_combine(e1, e2):
    a1r, a1i, b1r, b1i = e1
    a2r, a2i, b2r, b2i = e2
    ar = a2r * a1r - a2i * a1i
    ai = a2r * a1i + a2i * a1r
    br = a2r * b1r - a2i * b1i + b2r
    bi = a2r * b1i + a2i * b1r + b2i
    return (ar, ai, br, bi)


def s5_mixer(u, h0_re, h0_im, lam_re, lam_im, log_dt, b_re, b_im, c_re, c_im, d_s5, w_glu, b_glu):
    bsz, L, _ = u.shape
    uf = u.astype(jnp.float32).reshape(bsz, L, S5_GROUPS, S5_CH)
    dt = jnp.exp(log_dt.astype(jnp.float32))[:, None]
    lr = lam_re.astype(jnp.float32)
    li = lam_im.astype(jnp.float32)
    mag = jnp.exp(lr * dt)
    ab_re = mag * jnp.cos(li * dt)
    ab_im = mag * jnp.sin(li * dt)
    den = lr * lr + li * li
    nr, ni = ab_re - 1.0, ab_im
    f_re = (nr * lr + ni * li) / den
    f_im = (ni * lr - nr * li) / den
    br = b_re.astype(jnp.float32)
    bi = b_im.astype(jnp.float32)
    bb_re = f_re[..., None] * br - f_im[..., None] * bi
    bb_im = f_re[..., None] * bi + f_im[..., None] * br
    bu_re = jnp.einsum('gpc,blgc->blgp', bb_re, uf)
    bu_im = jnp.einsum('gpc,blgc->blgp', bb_im, uf)
    a_re = jnp.broadcast_to(ab_re, bu_re.shape)
    a_im = jnp.broadcast_to(ab_im, bu_im.shape)
    A_re, A_im, s_re, s_im = lax.associative_scan(
        _complex_affine_combine, (a_re, a_im, bu_re, bu_im), axis=1)
    h0r = h0_re.astype(jnp.float32)[:, None]
    h0i = h0_im.astype(jnp.float32)[:, None]
    h_re = A_re * h0r - A_im * h0i + s_re
    h_im = A_re * h0i + A_im * h0r + s_im
    y = (jnp.einsum('gcp,blgp->blgc', c_re.astype(jnp.float32), h_re)
         - jnp.einsum('gcp,blgp->blgc', c_im.astype(jnp.float32), h_im)
         + d_s5.astype(jnp.float32).reshape(S5_GROUPS, S5_CH) * uf)
    y = y.reshape(bsz, L, D_S5)
    g = jax.nn.gelu(y)
    out = g * jax.nn.sigmoid(g @ w_glu.astype(jnp.float32) + b_glu.astype(jnp.float32))
    return out.astype(u.dtype), h_re[:, -1], h_im[:, -1]


def ssd_chunked(x, dt, A, B, C, h0):
    bsz, L = x.shape[:2]
    Q = min(SSD_CHUNK, L)
    pad = (-L) % Q
    if pad:
        padw = lambda t: jnp.pad(t, [(0, 0), (0, pad)] + [(0, 0)] * (t.ndim - 2))
        x, dt, B, C = padw(x), padw(dt), padw(B), padw(C)
    nc = (L + pad) // Q
    x = x.reshape(bsz, nc, Q, SSD_GROUPS, SSD_HPG, SSD_HEAD_DIM)
    dt = dt.reshape(bsz, nc, Q, SSD_GROUPS, SSD_HPG)
    B = B.reshape(bsz, nc, Q, SSD_GROUPS, SSD_STATE)
    C = C.reshape(bsz, nc, Q, SSD_GROUPS, SSD_STATE)
    a = jnp.moveaxis(dt * A, 2, -1)
    a_cum = jnp.cumsum(a, axis=-1)
    mask = jnp.tril(jnp.ones((Q, Q), dtype=bool))
    seg = a_cum[..., :, None] - a_cum[..., None, :]
    Lmat = jnp.exp(jnp.where(mask, seg, -jnp.inf))
    xdt = x * dt[..., None]
    cb = jnp.einsum('bclgn,bcsgn->bcgls', C, B)
    y_diag = jnp.einsum('bcgls,bcgrls,bcsgrp->bclgrp', cb, Lmat, xdt)
    decay_states = jnp.exp(a_cum[..., -1:] - a_cum)
    states = jnp.einsum('bclgn,bcgrl,bclgrp->bcgrpn', B, decay_states, xdt)
    chunk_decay = jnp.exp(a_cum[..., -1])

    def step(h, inp):
        dec, st = inp
        return dec[..., None, None] * h + st, h

    h_last, h_prev = lax.scan(step, h0, (jnp.moveaxis(chunk_decay, 1, 0), jnp.moveaxis(states, 1, 0)))
    h_prev = jnp.moveaxis(h_prev, 0, 1)
    y_off = jnp.einsum('bclgn,bcgrpn,bcgrl->bclgrp', C, h_prev, jnp.exp(a_cum))
    y = (y_diag + y_off).reshape(bsz, nc * Q, SSD_GROUPS, SSD_HPG, SSD_HEAD_DIM)[:, :L]
    return y, h_last


def ssd_mixer(z, xbc, dt_raw, h0, conv_buf, conv_w, conv_b, a_log, dt_bias, d_ssd, norm_w):
    bsz, L, _ = xbc.shape
    xpad = jnp.concatenate([conv_buf.astype(xbc.dtype), xbc], axis=1)
    new_conv = xpad[:, -(SSD_CONV - 1):]
    conv = conv_b + sum(xpad[:, k:k + L] * conv_w[k] for k in range(SSD_CONV))
    conv = jax.nn.silu(conv.astype(jnp.float32))
    xs = conv[..., :D_SSD].reshape(bsz, L, SSD_GROUPS, SSD_HPG, SSD_HEAD_DIM)
    Bm = conv[..., D_SSD:D_SSD + SSD_GROUPS * SSD_STATE].reshape(bsz, L, SSD_GROUPS, SSD_STATE)
    Cm = conv[..., D_SSD + SSD_GROUPS * SSD_STATE:].reshape(bsz, L, SSD_GROUPS, SSD_STATE)
    dt = jax.nn.softplus(dt_raw.astype(jnp.float32) + dt_bias.astype(jnp.float32))
    dt = dt.reshape(bsz, L, SSD_GROUPS, SSD_HPG)
    A = -jnp.exp(a_log.astype(jnp.float32)).reshape(SSD_GROUPS, SSD_HPG)
    h0g = h0.astype(jnp.float32).reshape(bsz, SSD_GROUPS, SSD_HPG, SSD_HEAD_DIM, SSD_STATE)
    y, h_last = ssd_chunked(xs, dt, A, Bm, Cm, h0g)
    y = y + d_ssd.astype(jnp.float32).reshape(SSD_GROUPS, SSD_HPG, 1) * xs
    y = y.reshape(bsz, L, D_SSD) * jax.nn.silu(z.astype(jnp.float32))
    yg = y.reshape(bsz, L, SSD_GROUPS, D_SSD // SSD_GROUPS)
    yg = yg * lax.rsqrt(jnp.mean(yg * yg, axis=-1, keepdims=True) + EPS)
    y = yg.reshape(bsz, L, D_SSD) * norm_w.astype(jnp.float32)
    h_last = h_last.reshape(bsz, SSD_HEADS, SSD_HEAD_DIM, SSD_STATE)
    return y.astype(z.dtype), h_last, new_conv


def hybrid_layer(x, h5_re, h5_im, h_ssd, conv_buf, p):
    h = rmsnorm(x, p['norm_mix_w'])
    proj = h @ p['w_in']
    o1, o2, o3 = D_S5, D_S5 + D_SSD, D_S5 + D_SSD + D_CONV
    u, z, xbc, dt_raw = proj[..., :o1], proj[..., o1:o2], proj[..., o2:o3], proj[..., o3:]
    y5, n5_re, n5_im = s5_mixer(u, h5_re, h5_im, p['s5_lam_re'], p['s5_lam_im'], p['s5_log_dt'],
                                p['s5_b_re'], p['s5_b_im'], p['s5_c_re'], p['s5_c_im'], p['s5_d'],
                                p['s5_w_glu'], p['s5_b_glu'])
    y5 = rmsnorm(y5, p['s5_norm_w'])
    yssd, n_ssd, n_conv = ssd_mixer(z, xbc, dt_raw, h_ssd, conv_buf, p['ssd_conv_w'], p['ssd_conv_b'],
                                    p['ssd_a_log'], p['ssd_dt_bias'], p['ssd_d'], p['ssd_norm_w'])
    x = x + jnp.concatenate([y5, yssd], axis=-1) @ p['w_out']
    hf = rmsnorm(x, p['norm_ffn_w'])
    x = x + jnp.square(jax.nn.relu(hf @ p['w_ff1'])) @ p['w_ff2']
    return x, n5_re, n5_im, n_ssd, n_conv


def setup_inputs(seed: int = 0) -> dict:
    key = jax.random.key(seed)
    ks = jax.random.split(key, 32)
    nrm = lambda k, shape, s: jax.random.normal(k, shape, jnp.float32) * s
    Ld = DEPTH
    dt_ssd = jnp.exp(jax.random.uniform(ks[20], (Ld, SSD_HEADS), jnp.float32, math.log(1e-3), math.log(1e-1)))
    return {
        "x_prompt": nrm(ks[0], (BATCH, SEQ, D_MODEL), 1.0),
        "x_sample": nrm(ks[1], (DEC_BATCH, DEC_SEQ, D_MODEL), 1.0),
        "state_s5_re": nrm(ks[2], (Ld, DEC_BATCH, S5_GROUPS, S5_STATE), 0.5),
        "state_s5_im": nrm(ks[3], (Ld, DEC_BATCH, S5_GROUPS, S5_STATE), 0.5),
        "state_ssd": nrm(ks[4], (Ld, DEC_BATCH, SSD_HEADS, SSD_HEAD_DIM, SSD_STATE), 0.1),
        "state_conv": nrm(ks[5], (Ld, DEC_BATCH, SSD_CONV - 1, D_CONV), 1.0),
        "norm_mix_w": 1.0 + nrm(ks[6], (Ld, D_MODEL), 0.02),
        "w_in": nrm(ks[7], (Ld, D_MODEL, D_IN_PROJ), D_MODEL ** -0.5),
        "s5_lam_re": -0.5 + nrm(ks[8], (Ld, S5_GROUPS, S5_STATE), 0.01),
        "s5_lam_im": jnp.broadcast_to(math.pi * jnp.arange(S5_STATE, dtype=jnp.float32), (Ld, S5_GROUPS, S5_STATE))
                      + nrm(ks[9], (Ld, S5_GROUPS, S5_STATE), 0.01),
        "s5_log_dt": jax.random.uniform(ks[10], (Ld, S5_GROUPS), jnp.float32, math.log(1e-3), math.log(1e-1)),
        "s5_b_re": nrm(ks[11], (Ld, S5_GROUPS, S5_STATE, S5_CH), (2 * S5_CH) ** -0.5),
        "s5_b_im": nrm(ks[12], (Ld, S5_GROUPS, S5_STATE, S5_CH), (2 * S5_CH) ** -0.5),
        "s5_c_re": nrm(ks[13], (Ld, S5_GROUPS, S5_CH, S5_STATE), (2 * S5_STATE) ** -0.5),
        "s5_c_im": nrm(ks[14], (Ld, S5_GROUPS, S5_CH, S5_STATE), (2 * S5_STATE) ** -0.5),
        "s5_d": nrm(ks[15], (Ld, D_S5), 1.0),
        "s5_w_glu": nrm(ks[16], (Ld, D_S5, D_S5), D_S5 ** -0.5),
        "s5_b_glu": nrm(ks[17], (Ld, D_S5), 0.01),
        "s5_norm_w": 1.0 + nrm(ks[18], (Ld, D_S5), 0.02),
        "ssd_conv_w": nrm(ks[19], (Ld, SSD_CONV, D_CONV), SSD_CONV ** -0.5),
        "ssd_conv_b": nrm(ks[21], (Ld, D_CONV), 0.01),
        "ssd_a_log": jnp.log(jax.random.uniform(ks[22], (Ld, SSD_HEADS), jnp.float32, 1.0, 16.0)),
        "ssd_dt_bias": dt_ssd + jnp.log(-jnp.expm1(-dt_ssd)),
        "ssd_d": 1.0 + nrm(ks[23], (Ld, SSD_HEADS), 0.02),
        "ssd_norm_w": 1.0 + nrm(ks[24], (Ld, D_SSD), 0.02),
        "w_out": nrm(ks[25], (Ld, D_MIX, D_MODEL), D_MIX ** -0.5),
        "norm_ffn_w": 1.0 + nrm(ks[26], (Ld, D_MODEL), 0.02),
        "w_ff1": nrm(ks[27], (Ld, D_MODEL, D_FF), D_MODEL ** -0.5),
        "w_ff2": nrm(ks[28], (Ld, D_FF, D_MODEL), D_FF ** -0.5),
        "norm_final_w": 1.0 + nrm(ks[29], (D_MODEL,), 0.02),
    }


def reference(x_prompt, x_sample, state_s5_re, state_s5_im, state_ssd, state_conv,
              norm_mix_w, w_in, s5_lam_re, s5_lam_im, s5_log_dt, s5_b_re, s5_b_im, s5_c_re, s5_c_im,
              s5_d, s5_w_glu, s5_b_glu, s5_norm_w, ssd_conv_w, ssd_conv_b, ssd_a_log, ssd_dt_bias,
              ssd_d, ssd_norm_w, w_out, norm_ffn_w, w_ff1, w_ff2, norm_final_w):
    bp = x_prompt.shape[0]
    xp, xs = x_prompt, x_sample
    np_re, np_im, np_ssd, np_conv = [], [], [], []
    ns_re, ns_im, ns_ssd, ns_conv = [], [], [], []
    for l in range(DEPTH):
        p = dict(norm_mix_w=norm_mix_w[l], w_in=w_in[l], s5_lam_re=s5_lam_re[l], s5_lam_im=s5_lam_im[l],
                 s5_log_dt=s5_log_dt[l], s5_b_re=s5_b_re[l], s5_b_im=s5_b_im[l], s5_c_re=s5_c_re[l],
                 s5_c_im=s5_c_im[l], s5_d=s5_d[l], s5_w_glu=s5_w_glu[l], s5_b_glu=s5_b_glu[l],
                 s5_norm_w=s5_norm_w[l], ssd_conv_w=ssd_conv_w[l], ssd_conv_b=ssd_conv_b[l],
                 ssd_a_log=ssd_a_log[l], ssd_dt_bias=ssd_dt_bias[l], ssd_d=ssd_d[l], ssd_norm_w=ssd_norm_w[l],
                 w_out=w_out[l], norm_ffn_w=norm_ffn_w[l], w_ff1=w_ff1[l], w_ff2=w_ff2[l])
        z5 = jnp.zeros((bp, S5_GROUPS, S5_STATE), jnp.float32)
        zssd = jnp.zeros((bp, SSD_HEADS, SSD_HEAD_DIM, SSD_STATE), jnp.float32)
        zconv = jnp.zeros((bp, SSD_CONV - 1, D_CONV), x_prompt.dtype)
        xp, a, b, c, d = hybrid_layer(xp, z5, z5, zssd, zconv, p)
        np_re.append(a); np_im.append(b); np_ssd.append(c); np_conv.append(d)
        xs, a, b, c, d = hybrid_layer(xs, state_s5_re[l], state_s5_im[l], state_ssd[l], state_conv[l], p)
        ns_re.append(a); ns_im.append(b); ns_ssd.append(c); ns_conv.append(d)
    y_prompt = rmsnorm(xp, norm_final_w)
    y_sample = rmsnorm(xs, norm_final_w)
    return (y_prompt, y_sample,
            jnp.stack(np_re), jnp.stack(np_im), jnp.stack(np_ssd), jnp.stack(np_conv),
            jnp.stack(ns_re), jnp.stack(ns_im), jnp.stack(ns_ssd), jnp.stack(ns_conv))
```

```python
import math
from contextlib import ExitStack

import numpy as np
import concourse.bass as bass
import concourse.mybir as mybir
from concourse.bass_utils import run_bass_kernel_spmd

F32 = mybir.dt.float32
BF16 = mybir.dt.bfloat16
I32 = mybir.dt.int32
AF = mybir.ActivationFunctionType
ALU = mybir.AluOpType
AX = mybir.AxisListType

T = 2048
NS = 16
FAST = ('dense', 'glu', 'inproj', 'p1', 'ssd', 'smp', 'x', 'y', 'kmat')
EPS = 1e-5
TWO_PI = 2.0 * math.pi
GELU_C = 2.0 * math.sqrt(2.0 / math.pi)

PC_GMIX, PC_S5N, PC_BGLU, PC_D5, PC_SSDN, PC_CW, PC_CB, PC_FFN, PC_JV, PC_N = 0, 8, 16, 24, 32, 40, 88, 100, 108, 117
PB_FIN, PB_DTB, PB_ALOG, PB_DSSD, PB_N = 0, 1024, 1040, 1056, 1072
L1_LRE, L1_LIM, L1_LDT, L1_CRE, L1_CIM, L1_BRE, L1_BIM, L1_N = 0, 32, 64, 96, 1120, 2144, 3168, 4192


SPLIT = {}


def _keys(x):
    if isinstance(x, str):
        return [x]
    name = x.name
    sp = SPLIT.get(name)
    if sp is None:
        return [name]
    per, sub = sp
    try:
        ap = x.ap
        start = int(x.offset) % per
        ext = 1
        for (st_, cnt) in ap[1:]:
            ext += (cnt - 1) * abs(st_)
    except Exception:
        return ["%s#%d" % (name, i) for i in range(per // sub)]
    lo, hi = start // sub, min(per - 1, start + ext - 1) // sub
    return ["%s#%d" % (name, i) for i in range(lo, hi + 1)]


class Emit:
    ENG = ['pe', 'dve', 'act', 'pool', 'sp']

    def __init__(self, nc):
        self.nc = nc
        self.prog = {e: [] for e in self.ENG}
        self.sem = {e: nc.alloc_semaphore('c_' + e) for e in self.ENG}
        self.cnt = {e: 0 for e in self.ENG}
        self.waited = {e: {} for e in self.ENG}
        self.lastw = {}
        self.readers = {}
        self.dsem = {}
        self.dcnt = {}
        self.deng = {}
        self.nwait = 0
        self.pe_fast = False
        self.skip_groups = set()

    def _deps(self, reads, writes):
        deps = []
        for k in reads:
            if k in self.lastw:
                deps.append(self.lastw[k])
        for k in writes:
            if k in self.lastw:
                deps.append(self.lastw[k])
            deps.extend(self.readers.get(k, []))
        return deps

    def _emit_waits(self, eng, deps):
        w = self.waited[eng]
        need = {}
        for (s, v) in deps:
            if w.get(s, 0) < v and need.get(s, 0) < v:
                need[s] = v
        for s, v in need.items():
            w[s] = v
            self.nwait += 1
            self.prog[eng].append(('wait', s, v))

    def _record(self, dep, reads, writes):
        for k in writes:
            self.lastw[k] = dep
            self.readers[k] = []
        for k in reads:
            if k not in writes:
                self.readers.setdefault(k, []).append(dep)

    def op(self, eng, fn, reads=(), writes=()):
        reads = [k for r in reads for k in _keys(r)]
        writes = [k for r in writes for k in _keys(r)]
        deps = self._deps(reads, writes)
        if eng == 'pe' and self.pe_fast:
            deps = [d for d in deps if d[0] is not self.sem['pe']]
        self._emit_waits(eng, deps)
        self.cnt[eng] += 1
        dep = (self.sem[eng], self.cnt[eng])
        self.prog[eng].append(('op', fn, self.sem[eng], 1))
        self._record(dep, reads, writes)

    def dma(self, eng, out, in_, reads=(), writes=(), **kw):
        reads = [k for r in reads for k in _keys(r)]
        g = _keys(writes[0])[0].split('#')[0]
        writes = [k for r in writes for k in _keys(r)]
        self._emit_waits(eng, self._deps(reads, writes))
        if g not in self.dsem:
            self.dsem[g] = self.nc.alloc_semaphore('d_%d' % len(self.dsem))
            self.dcnt[g] = 0
        self.dcnt[g] += 16
        self.deng[g] = eng
        dep = (self.dsem[g], self.dcnt[g])
        self.prog[eng].append(('op', lambda e: e.dma_start(out=out, in_=in_, **kw), self.dsem[g], 16))
        self._record(dep, reads, writes)

    def barrier(self, exclude=None):
        engs = [e for e in self.ENG if e != exclude]
        deps = [(self.sem[e], self.cnt[e]) for e in engs if self.cnt[e] > 0]
        for g, s in self.dsem.items():
            if self.deng.get(g) != exclude and g not in self.skip_groups:
                deps.append((s, self.dcnt[g]))
        for e in engs:
            if e != 'pe':
                self._emit_waits(e, deps)

    def finish(self, eng='sp'):
        deps = list(self.lastw.values())
        for r in self.readers.values():
            deps.extend(r)
        self._emit_waits(eng, deps)

    def build(self):
        nc = self.nc
        prog = self.prog

        def run(e, lst):
            for it in lst:
                if it[0] == 'wait':
                    e.wait_ge(it[1], it[2])
                else:
                    it[1](e).then_inc(it[2], it[3])

        with nc.Block() as block:
            @block.sync
            def _(e):
                run(e, prog['sp'])

            @block.tensor
            def _(e):
                run(e, prog['pe'])

            @block.vector
            def _(e):
                run(e, prog['dve'])

            @block.scalar
            def _(e):
                run(e, prog['act'])

            @block.gpsimd
            def _(e):
                run(e, prog['pool'])


def build_program(debug=()):
    nc = bass.Bass("TRN2", target_bir_lowering=False)
    E = Emit(nc)
    st = ExitStack()

    def din(name, shape):
        return nc.dram_tensor(name, list(shape), F32, kind="ExternalInput")

    def dout(name, shape):
        return nc.dram_tensor(name, list(shape), F32, kind="ExternalOutput")

    xp_tm = din("xp_tm", [T, 1024]); xp_T = din("xp_T", [1024, T])
    xs_tm = din("xs_tm", [NS, 1024]); xs_T = din("xs_T", [1024, NS])
    pcol_d = din("pcol", [128, PC_N]); pbc_d = din("pbc", [128, PB_N]); p16_d = din("p16", [16, 1027])
    cst_d = din("cst", [128, 256])
    l1_d = din("l1pack", [128, L1_N]); g_d = din("gpack", [64, 129]); b2_d = din("b2pack", [128, 2048])
    h0l1_d = din("h0l1", [128, 1024])
    ssd0_d = din("ssd0", [NS, 1024, 128]); conv0T_d = din("conv0T", [128, 12 * 3 * NS]); conv0_d = din("conv0", [NS, 3 * 1536])
    w_in_d = din("w_in", [1024, 3600]); w_glu_d = din("w_glu", [1024, 1024]); w_out_d = din("w_out", [2048, 1024])
    w_ff1_d = din("w_ff1", [1024, 4096]); w_ff2_d = din("w_ff2", [4096, 1024])
    scr_d = nc.dram_tensor("scr_pf", [64, 1024], F32, kind="Internal")
    wb_in = nc.dram_tensor("wb_in", [1024, 3600], BF16, kind="Internal")
    wb_glu = nc.dram_tensor("wb_glu", [1024, 1024], BF16, kind="Internal")
    wb_out = nc.dram_tensor("wb_out", [2048, 1024], BF16, kind="Internal")
    wb_ff1 = nc.dram_tensor("wb_ff1", [1024, 4096], BF16, kind="Internal")
    wb_ff2 = nc.dram_tensor("wb_ff2", [4096, 1024], BF16, kind="Internal")

    y_p_d = dout("y_p", [T, 1024]); y_s_d = dout("y_s", [NS, 1024])
    np5_d = dout("np5", [128, 64]); np_ssdT_d = dout("np_ssdT", [128, 1024]); np_convT_d = dout("np_convT", [128, 36])
    ns5_d = dout("ns5", [128, 1024]); ns_ssd_d = dout("ns_ssd", [NS, 1024, 128])
    ns_conv_a_d = dout("ns_conv_a", [NS, 2 * 1536]); ns_conv_bT_d = dout("ns_conv_bT", [128, 12 * NS])
    dbg_d = {}

    uniq = [0]

    def sbx(stack, name, shape, dt=F32):
        uniq[0] += 1
        return stack.enter_context(nc.sbuf_tensor("s%d_%s" % (uniq[0], name), list(shape), dt))

    def sb(name, shape, dt=F32):
        return sbx(st, name, shape, dt)

    banks = [st.enter_context(nc.psum_tensor("pb%d" % i, [128, 512], F32)) for i in range(8)]
    bank_i = [0]

    def bank(avoid=None):
        b = banks[bank_i[0] % 8]
        bank_i[0] += 1
        if avoid is not None and b.name == avoid.name:
            b = banks[bank_i[0] % 8]
            bank_i[0] += 1
        return b

    def aps(*xs):
        return [x for x in xs if x is not None and not isinstance(x, (int, float))]

    def TT(eng, out, in0, in1, op):
        E.op(eng, lambda e: e.tensor_tensor(out=out, in0=in0, in1=in1, op=op), reads=aps(in0, in1), writes=[out])

    def TS(eng, out, in0, s1, s2=None, op0=ALU.mult, op1=None):
        kw = {}
        if op1 is not None:
            kw['op1'] = op1
        E.op(eng, lambda e: e.tensor_scalar(out=out, in0=in0, scalar1=s1, scalar2=s2, op0=op0, **kw),
             reads=aps(in0, s1, s2), writes=[out])

    def STT(out, in0, scalar, in1, op0, op1, eng='dve'):
        E.op(eng, lambda e: e.scalar_tensor_tensor(out=out, in0=in0, scalar=scalar, in1=in1, op0=op0, op1=op1),
             reads=aps(in0, scalar, in1), writes=[out])

    def ACT(out, in_, func, bias=None, scale=1.0, accum_out=None):
        kw = {}
        if bias is not None:
            kw['bias'] = bias
        if accum_out is not None:
            kw['accum_out'] = accum_out
        E.op('act', lambda e: e.activation(out=out, in_=in_, func=func, scale=scale, **kw),
             reads=aps(in_, bias, scale), writes=aps(out, accum_out))

    def CP(eng, out, in_):
        if eng == 'act':
            E.op('act', lambda e: e.copy(out=out, in_=in_), reads=[in_], writes=[out])
        else:
            E.op(eng, lambda e: e.tensor_copy(out=out, in_=in_), reads=[in_], writes=[out])

    def MSET(eng, out, val):
        E.op(eng, lambda e: e.memset(out, val), writes=[out])

    def MM(out, lhsT, rhs, start, stop, tp=None):
        kw = {}
        if tp is not None:
            kw['tile_position'] = tp
        E.op('pe', lambda e: e.matmul(out, lhsT=lhsT, rhs=rhs, start=start, stop=stop, **kw),
             reads=[lhsT, rhs], writes=[out])

    def TR(out, in_, ident):
        E.op('pe', lambda e: e.transpose(out, in_, ident), reads=[in_, ident], writes=[out])

    dq = ['sp']

    def LD(out, in_, eng=None, **kw):
        E.dma(eng or dq[0], out, in_, writes=[out], **kw)

    def STo(out, in_, eng=None):
        E.dma(eng or dq[0], out, in_, reads=[in_], writes=[out])

    def RED(out, in_, eng='dve'):
        E.op(eng, lambda e: e.tensor_reduce(out=out, in_=in_, axis=AX.X, op=ALU.add), reads=[in_], writes=[out])

    def RECIP(out, in_):
        E.op('dve', lambda e: e.reciprocal(out=out, in_=in_), reads=[in_], writes=[out])

    def rstd_from(out, in_, n, tmp):
        TS('dve', tmp, in_, 1.0 / n, EPS, ALU.mult, ALU.add)
        ACT(tmp, tmp, AF.Ln)
        ACT(out, tmp, AF.Exp, scale=-0.5)

    def bcl(ap, shape):
        return ap.rearrange("p (f o) -> p f o", o=1).to_broadcast(list(shape))

    def bcm(ap, shape):
        return ap.rearrange("p (o f) -> p o f", o=1).to_broadcast(list(shape))

    def wunit(ring, ri, dram, k0, c0, ncols=512, bf=None):
        t = ring[ri[0] % len(ring)]
        ri[0] += 1
        if bf is None:
            src = dram.ap()[k0:k0 + 1024, c0:c0 + ncols].rearrange("(kt p) c -> p kt c", p=128)
            E.dma('pool', t[:, :, 0:ncols], src, writes=[t])
        else:
            src = bf.ap()[k0:k0 + 1024, c0:c0 + ncols].rearrange("(kt p) c -> p kt c", p=128)
            E.dma('sp', t[:, :, 0:ncols], src, reads=[bf.ap()], writes=[t])
        return t

    def load_hnT(xT_dram, c0, n, hnT, rbc=None):
        E.dma('pool', hnT[:, :, 0:n], xT_dram.ap()[:, c0:c0 + n].rearrange("(kt p) t -> p kt t", p=128), writes=[hnT])
        if rbc is not None:
            for kt in range(8):
                STT(hnT[:, kt, 0:n], hnT[:, kt, 0:n], pcol[:, PC_GMIX + kt:PC_GMIX + kt + 1], rbc, ALU.mult, ALU.mult)
        else:
            TT('dve', hnT[:, :, 0:n], hnT[:, :, 0:n], bcl(pcol[:, PC_GMIX:PC_GMIX + 8], [128, 8, n]), ALU.mult)

    def dbg(name, ap, shape, stack=None, dt=BF16):
        if name in debug:
            d = nc.dram_tensor("dbg_" + name, list(shape), dt, kind="ExternalOutput")
            dbg_d[name] = d
            STo(d.ap(), ap)

    def convert_weights():
        for src, dst, rows in ((w_glu_d, wb_glu, 1024), (w_in_d, wb_in, 1024), (w_out_d, wb_out, 2048),
                               (w_ff1_d, wb_ff1, 1024), (w_ff2_d, wb_ff2, 4096)):
            for r0 in range(0, rows, 512):
                E.dma('pool', dst.ap()[r0:r0 + 512, :], src.ap()[r0:r0 + 512, :], writes=[dst.ap()])
            E.skip_groups.add(dst.ap().name)

    cst = sb("cst", [128, 256]); ident_f = cst[:, 0:128]; tri_f = cst[:, 128:256]
    ident_b = sb("ident_b", [128, 128], BF16)
    ones_f = sb("ones_f", [128, 128])
    pcol = sb("pcol", [128, PC_N])
    uT = [sb("uT%d" % o, [128, T + NS], BF16) for o in range(8)]
    rstdbc_all = sb("rstdbc_all", [128, T]); rstdbc_s = sb("rstdbc_s", [128, NS])
    lam18 = sb("lam18", [128, 4, 32])
    HT = sb("HT", [128, 1024]); HTb = sb("HTb", [128, 1024], BF16)

    LD(cst[:], cst_d.ap()); LD(pcol[:], pcol_d.ap())
    CP('dve', ident_b[:], ident_f)
    MSET('pool', ones_f[:], 1.0)

    with ExitStack() as st1:
        E.pe_fast = 'p1' in FAST
        hnT = [sbx(st1, "p1hn%d" % i, [128, 8, 512], BF16) for i in range(2)]
        sqb = [sbx(st1, "p1sq%d" % i, [128, 8, 512], BF16) for i in range(2)]
        ones_b = sbx(st1, "p1ones", [128, 128], BF16)
        CP('dve', ones_b[:], ones_f[:])
        ring1 = [sbx(st1, "p1w%d" % i, [128, 8, 512], BF16) for i in range(2)]
        r1i = [0]
        wu = [wunit(ring1, r1i, w_in_d, 0, 0), wunit(ring1, r1i, w_in_d, 0, 512)]
        rts = [sbx(st1, "p1rt%d" % i, [128, 512]) for i in range(2)]

        def p1_prep(blk):
            smp = blk == 4
            n = NS if smp else 512
            hn = hnT[blk % 2]; sq = sqb[blk % 2]
            E.dma('pool', hn[:, :, 0:n], (xs_T if smp else xp_T).ap()[:, (0 if smp else blk * 512):(0 if smp else blk * 512) + n]
                  .rearrange("(kt p) t -> p kt t", p=128), writes=[hn])
            ACT(sq[:, :, 0:n], hn[:, :, 0:n], AF.Square)
            ps = bank()
            for kt in range(8):
                MM(ps[:, 0:n], lhsT=ones_b[:], rhs=sq[:, kt, 0:n], start=(kt == 0), stop=(kt == 7))
            rbc = rstdbc_s[:, :] if smp else rstdbc_all[:, blk * 512:(blk + 1) * 512]
            rstd_from(rbc, ps[:, 0:n], 1024.0, rts[blk % 2][:, 0:n])
            for kt in range(8):
                STT(hn[:, kt, 0:n], hn[:, kt, 0:n], pcol[:, PC_GMIX + kt:PC_GMIX + kt + 1], rbc, ALU.mult, ALU.mult)

        def p1_mm(blk):
            smp = blk == 4
            n = NS if smp else 512
            hn = hnT[blk % 2]
            c0 = T if smp else blk * 512
            for o in range(8):
                ps = bank()
                for kt in range(8):
                    MM(ps[:, 0:n], lhsT=wu[o // 4][:, kt, (o % 4) * 128:(o % 4) * 128 + 128], rhs=hn[:, kt, 0:n],
                       start=(kt == 0), stop=(kt == 7))
                if smp:
                    CP('dve', uT[o][:, c0:c0 + n], ps[:, 0:n])
                else:
                    CP('act' if o % 2 == 0 else 'dve', uT[o][:, 0:T].rearrange("p (s k) -> p k s", s=8)[:, 64 * blk:64 * blk + 64, :],
                       ps[:, 0:512].rearrange("p (k s) -> p k s", s=8))

        p1_prep(0)
        for blk in range(5):
            if blk < 4:
                p1_prep(blk + 1)
            p1_mm(blk)
    E.pe_fast = False
    E.barrier()
    if 'u' in debug:
        d = nc.dram_tensor("dbg_u", [128, 8, T + NS], BF16, kind="ExternalOutput"); dbg_d['u'] = d
        for o in range(8):
            STo(d.ap()[:, o, :], uT[o][:])

    def powers(stk, P, Fd, lre, lim, ldt, ar, ai, tg):
        dtt = sbx(stk, tg + "dt", [P, Fd]); lrdt = sbx(stk, tg + "lrdt", [P, Fd]); lidt = sbx(stk, tg + "lidt", [P, Fd])
        mag = sbx(stk, tg + "mag", [P, 9, Fd]); r = sbx(stk, tg + "r", [P, 9, Fd]); rn = sbx(stk, tg + "rn", [P, 9, Fd])
        ri = sbx(stk, tg + "ri", [P, 9, Fd], I32); ng = sbx(stk, tg + "ng", [P, 9, Fd])
        jb = bcl(pcol[0:P, PC_JV:PC_JV + 9], [P, 9, Fd])
        ACT(dtt[:], ldt, AF.Exp)
        TT('dve', lrdt[:], lre, dtt[:], ALU.mult)
        TT('dve', lidt[:], lim, dtt[:], ALU.mult)
        TT('dve', mag[:], bcm(lrdt[:], [P, 9, Fd]), jb, ALU.mult)
        ACT(mag[:], mag[:], AF.Exp)
        for dst, off in ((ai, 0.0), (ar, 0.25)):
            TT('dve', r[:], bcm(lidt[:], [P, 9, Fd]), jb, ALU.mult)
            TS('dve', r[:], r[:], 1.0 / TWO_PI, off, ALU.mult, ALU.add)
            CP('dve', ri[:], r[:])
            CP('dve', rn[:], ri[:])
            TT('dve', r[:], r[:], rn[:], ALU.subtract)
            TS('dve', ng[:], r[:], 0.0, None, ALU.is_lt)
            TT('dve', r[:], r[:], ng[:], ALU.add)
            TS('dve', ng[:], r[:], 1.0, None, ALU.is_ge)
            TT('dve', r[:], r[:], ng[:], ALU.subtract)
            sc = (1.0 - 1e-6)
            TS('dve', r[:], r[:], -TWO_PI * sc, math.pi * sc, ALU.mult, ALU.add)
            ACT(rn[:], r[:], AF.Sin)
            TT('dve', dst, rn[:], mag[:], ALU.mult)

    def fcoef(stk, P, Fd, lre, lim, ar1, ai1, fr, fi, tg):
        den = sbx(stk, tg + "den", [P, Fd]); t1 = sbx(stk, tg + "t1", [P, Fd]); nr = sbx(stk, tg + "nr", [P, Fd])
        TT('dve', den[:], lre, lre, ALU.mult)
        TT('dve', t1[:], lim, lim, ALU.mult)
        TT('dve', den[:], den[:], t1[:], ALU.add)
        RECIP(den[:], den[:])
        TS('dve', nr[:], ar1, -1.0, None, ALU.add)
        TT('dve', fr, nr[:], lre, ALU.mult)
        TT('dve', t1[:], ai1, lim, ALU.mult)
        TT('dve', fr, fr, t1[:], ALU.add)
        TT('dve', fr, fr, den[:], ALU.mult)
        TT('dve', fi, ai1, lre, ALU.mult)
        TT('dve', t1[:], nr[:], lim, ALU.mult)
        TT('dve', fi, fi, t1[:], ALU.subtract)
        TT('dve', fi, fi, den[:], ALU.mult)

    stS5 = ExitStack()
    XH = [sbx(stS5, "XH%d" % h, [128, 2, 16, 257], BF16) for h in range(2)]
    Xs_sb = sbx(stS5, "Xs_sb", [128, 2, 32, NS])
    with ExitStack() as stW:
        WXt = sbx(stW, "WXt", [128, 8, 8, 2, 128], BF16)
        with ExitStack() as st0:
            gp = sbx(st0, "gp", [64, 129]); b2 = sbx(st0, "b2", [128, 2048])
            LD(gp[:], g_d.ap()); LD(b2[:], b2_d.ap())
            arG = sbx(st0, "arG", [64, 9, 64]); aiG = sbx(st0, "aiG", [64, 9, 64])
            frG = sbx(st0, "frG", [64, 64]); fiG = sbx(st0, "fiG", [64, 64])
            powers(st0, 64, 64, gp[:, 0:64], gp[:, 64:128], gp[:, 128:129].to_broadcast([64, 64]), arG[:], aiG[:], "G")
            fcoef(st0, 64, 64, gp[:, 0:64], gp[:, 64:128], arG[:, 1, :], aiG[:, 1, :], frG[:], fiG[:], "G")
            PfG = sbx(st0, "PfG", [64, 8, 2, 64]); tG = sbx(st0, "tG", [64, 8, 64])
            frb = bcm(frG[:], [64, 8, 64]); fib = bcm(fiG[:], [64, 8, 64])
            TT('dve', PfG[:, :, 0, :], arG[:, 0:8, :], frb, ALU.mult)
            TT('dve', tG[:], aiG[:, 0:8, :], fib, ALU.mult)
            TT('dve', PfG[:, :, 0, :], PfG[:, :, 0, :], tG[:], ALU.subtract)
            TT('dve', PfG[:, :, 1, :], arG[:, 0:8, :], fib, ALU.mult)
            TT('dve', tG[:], aiG[:, 0:8, :], frb, ALU.mult)
            TT('dve', PfG[:, :, 1, :], PfG[:, :, 1, :], tG[:], ALU.add)
            STo(scr_d.ap(), PfG[:].rearrange("p a b c -> p (a b c)"))
            PfL2 = sbx(st0, "PfL2", [128, 8, 16, 64])
            for g8 in range(8):
                src = bass.AP(scr_d, g8 * 1024, [[0, 16], [8 * 1024, 8], [64, 16], [1, 64]])
                E.dma('sp', PfL2[16 * g8:16 * g8 + 16, :, :, :], src, reads=[scr_d.ap()], writes=[PfL2])
            tA = sbx(st0, "tA", [128, 1024]); tB = sbx(st0, "tB", [128, 1024])
            tA2 = sbx(st0, "tA2", [128, 1024]); tB2 = sbx(st0, "tB2", [128, 1024])
            b2re = b2[:, 0:1024].rearrange("p (o g q) -> p o g q", o=8, g=2)
            b2im = b2[:, 1024:2048].rearrange("p (o g q) -> p o g q", o=8, g=2)
            for s in range(8):
                j = 7 - s
                pre = PfL2[:, :, 2 * j, :].rearrange("p o (g q) -> p o g q", g=1).to_broadcast([128, 8, 2, 64])
                pim = PfL2[:, :, 2 * j + 1, :].rearrange("p o (g q) -> p o g q", g=1).to_broadcast([128, 8, 2, 64])
                e1, ta, tb = ('dve', tA, tB) if s % 3 != 2 else ('pool', tA2, tB2)
                tAv = ta[:].rearrange("p (o g q) -> p o g q", o=8, g=2)
                tBv = tb[:].rearrange("p (o g q) -> p o g q", o=8, g=2)
                TT(e1, tAv, b2re, pre, ALU.mult)
                TT(e1, tBv, b2im, pim, ALU.mult)
                TT(e1, WXt[:, :, s, 0, :].rearrange("p o (g q) -> p o g q", g=2), tAv, tBv, ALU.subtract)
                TT(e1, tAv, b2im, pre, ALU.mult)
                TT(e1, tBv, b2re, pim, ALU.mult)
                TT(e1, WXt[:, :, s, 1, :].rearrange("p o (g q) -> p o g q", g=2), tAv, tBv, ALU.add)
        dbg('WXt', WXt[:].rearrange("p a b c d -> p (a b c d)"), [128, 16384])
        E.barrier()
        E.pe_fast = 'x' in FAST
        for o in range(8):
            psq = [bank() for _ in range(4)]
            for ri in range(2):
                for s in range(8):
                    for jj in range(4):
                        MM(psq[jj][:, ri * 256:(ri + 1) * 256], lhsT=WXt[32 * jj:32 * jj + 32, o, s, ri, :],
                           rhs=uT[o][32 * jj:32 * jj + 32, s * 256:(s + 1) * 256], start=(s == 0), stop=(s == 7), tp=(32 * jj, 0))
            for jj in range(4):
                pair = 4 * o + jj
                CP('dve', XH[pair // 16][:, :, pair % 16, 1:257], psq[jj][:, :].rearrange("p (r k) -> p r k", r=2))
        psx = [bank() for _ in range(4)]
        for pair in range(32):
            o, jj = pair // 4, pair % 4
            for ri in range(2):
                MM(psx[jj][:, (ri * 8 + o) * NS:(ri * 8 + o + 1) * NS], lhsT=WXt[32 * jj:32 * jj + 32, o, 7, ri, :],
                   rhs=uT[o][32 * jj:32 * jj + 32, T:T + NS], start=True, stop=True, tp=(32 * jj, 0))
        for jj in range(4):
            for ri in range(2):
                CP('dve', Xs_sb[:, ri, jj:32:4, :], psx[jj][:, ri * 128:(ri + 1) * 128].rearrange("p (a b) -> p a b", b=NS))
    dbg('X0', XH[0][:].rearrange("p a b c -> p (a b c)"), [128, 2 * 16 * 257])
    E.pe_fast = False
    E.barrier()

    WCz = sbx(stS5, "WCz", [128, 32, 9, 2, 32], BF16)
    BBz = sbx(stS5, "BBz", [128, 32, 2, 32], BF16)
    Kmat = [sbx(stS5, "Kmat%d" % o, [128, 8, 128], BF16) for o in range(8)]
    with ExitStack() as st0:
        l1 = sbx(st0, "l1", [128, L1_N])
        LD(l1[:], l1_d.ap())
        arL = sbx(st0, "arL", [128, 9, 32]); aiL = sbx(st0, "aiL", [128, 9, 32])
        frL = sbx(st0, "frL", [128, 32]); fiL = sbx(st0, "fiL", [128, 32])
        powers(st0, 128, 32, l1[:, L1_LRE:L1_LRE + 32], l1[:, L1_LIM:L1_LIM + 32], l1[:, L1_LDT:L1_LDT + 32], arL[:], aiL[:], "L")
        fcoef(st0, 128, 32, l1[:, L1_LRE:L1_LRE + 32], l1[:, L1_LIM:L1_LIM + 32], arL[:, 1, :], aiL[:, 1, :], frL[:], fiL[:], "L")
        CP('dve', lam18[:, 0, :], arL[:, 1, :]); CP('dve', lam18[:, 1, :], aiL[:, 1, :])
        CP('dve', lam18[:, 2, :], arL[:, 8, :]); CP('dve', lam18[:, 3, :], aiL[:, 8, :])
        czre = l1[:, L1_CRE:L1_CRE + 1024].rearrange("p (a c) -> p a c", c=32)
        czim = l1[:, L1_CIM:L1_CIM + 1024].rearrange("p (a c) -> p a c", c=32)
        bzre = l1[:, L1_BRE:L1_BRE + 1024].rearrange("p (a c) -> p a c", c=32)
        bzim = l1[:, L1_BIM:L1_BIM + 1024].rearrange("p (a c) -> p a c", c=32)
        tCs = [sbx(st0, "tC%d" % i, [128, 32, 32]) for i in range(2)]
        tDs = [sbx(st0, "tD%d" % i, [128, 32, 32]) for i in range(2)]
        for j in range(9):
            arb = bcl(arL[:, j, :], [128, 32, 32]); aib = bcl(aiL[:, j, :], [128, 32, 32])
            e1 = 'dve' if j % 3 != 2 else 'pool'
            tC, tD = (tCs[0], tDs[0]) if e1 == 'dve' else (tCs[1], tDs[1])
            TT(e1, tC[:], czre, arb, ALU.mult)
            TT(e1, tD[:], czim, aib, ALU.mult)
            TT(e1, WCz[:, :, j, 0, :], tC[:], tD[:], ALU.subtract)
            TT(e1, tC[:], czre, aib, ALU.mult)
            TT(e1, tD[:], czim, arb, ALU.mult)
            TT(e1, tC[:], tC[:], tD[:], ALU.add)
            TS(e1, WCz[:, :, j, 1, :], tC[:], -1.0)
        tC, tD = tCs[0], tDs[0]
        frb = bcl(frL[:], [128, 32, 32]); fib = bcl(fiL[:], [128, 32, 32])
        TT('dve', tC[:], bzre, frb, ALU.mult)
        TT('dve', tD[:], bzim, fib, ALU.mult)
        TT('dve', BBz[:, :, 0, :], tC[:], tD[:], ALU.subtract)
        TT('dve', tC[:], bzre, fib, ALU.mult)
        TT('dve', tD[:], bzim, frb, ALU.mult)
        TT('dve', BBz[:, :, 1, :], tC[:], tD[:], ALU.add)
        E.pe_fast = 'kmat' in FAST
        for o in range(8):
            ps = bank()
            MSET('pool', Kmat[o][:], 0.0)
            for jj in range(4):
                pair = 4 * o + jj
                for ri in range(2):
                    MM(ps[32 * jj:32 * jj + 32, 0:256].rearrange("p (a c) -> p a c", c=32),
                       lhsT=BBz[:, pair, ri, :], rhs=WCz[:, pair, 0:8, ri, :],
                       start=(ri == 0), stop=(ri == 1), tp=(0, 32 * jj))
            for jj in range(4):
                CP('dve', Kmat[o][32 * jj:32 * jj + 32, :, 32 * jj:32 * jj + 32],
                   ps[32 * jj:32 * jj + 32, 0:256].rearrange("p (a c) -> p a c", c=32))
            STT(Kmat[o][:, 0, :], ident_f, pcol[:, PC_D5 + o:PC_D5 + o + 1], Kmat[o][:, 0, :], ALU.mult, ALU.add)
    E.pe_fast = False
    dbg('lam18', lam18[:].rearrange("p a b -> p (a b)"), [128, 128], None, F32)
    dbg('Kmat0', Kmat[0][:].rearrange("p a b -> p (a b)"), [128, 1024])
    dbg('WCz', WCz[:].rearrange("p a b c d -> p (a b c d)"), [128, 18432])
    E.barrier()

    def gelu_evac(out, ps_ap, t1, t2):
        ACT(t1, ps_ap, AF.Square)
        TS('dve', t1, t1, 0.044715, 1.0, ALU.mult, ALU.add)
        TT('dve', t1, t1, ps_ap, ALU.mult)
        ACT(t2, t1, AF.Sigmoid, scale=GELU_C)
        TT('dve', out, t2, ps_ap, ALU.mult)

    with ExitStack() as st2:
        h0 = sbx(st2, "h0", [128, 2, 32, NS]); h0b = sbx(st2, "h0b", [128, 2, 32, NS], BF16)
        ns5 = sbx(st2, "ns5", [128, 2, 32, NS]); tS = sbx(st2, "tS", [128, 32, NS])
        LD(h0[:].rearrange("p a b c -> p (a b c)"), h0l1_d.ap())
        CP('dve', h0b[:], h0[:])
        ar1b = bcl(lam18[:, 0, :], [128, 32, NS]); ai1b = bcl(lam18[:, 1, :], [128, 32, NS])
        for ri in range(2):
            TT('dve', ns5[:, ri, :, :], h0[:, ri, :, :], ar1b, ALU.mult)
            TT('dve', ns5[:, ri, :, :], ns5[:, ri, :, :], Xs_sb[:, ri, :, :], ALU.add)
        TT('dve', tS[:], h0[:, 1, :, :], ai1b, ALU.mult)
        TT('dve', ns5[:, 0, :, :], ns5[:, 0, :, :], tS[:], ALU.subtract)
        TT('dve', tS[:], h0[:, 0, :, :], ai1b, ALU.mult)
        TT('dve', ns5[:, 1, :, :], ns5[:, 1, :, :], tS[:], ALU.add)
        STo(ns5_d.ap(), ns5[:].rearrange("p a b c -> p (a b c)"))
        convert_weights()
        np5 = sbx(st2, "np5", [128, 2, 32])
        with ExitStack() as stH:
            NL = 5
            Dl = [sbx(stH, "Dl%d" % l, [128, 2, 32]) for l in range(NL + 1)]
            dtmp = sbx(stH, "dtmp", [128, 32])
            CP('dve', Dl[0][:, 0, :], lam18[:, 2, :]); CP('dve', Dl[0][:, 1, :], lam18[:, 3, :])
            for l in range(1, NL + 1):
                TT('dve', Dl[l][:, 0, :], Dl[l - 1][:, 0, :], Dl[l - 1][:, 0, :], ALU.mult)
                TT('dve', dtmp[:], Dl[l - 1][:, 1, :], Dl[l - 1][:, 1, :], ALU.mult)
                TT('dve', Dl[l][:, 0, :], Dl[l][:, 0, :], dtmp[:], ALU.subtract)
                TT('dve', Dl[l][:, 1, :], Dl[l - 1][:, 0, :], Dl[l - 1][:, 1, :], ALU.mult)
                TS('dve', Dl[l][:, 1, :], Dl[l][:, 1, :], 2.0)
            Yl = [None] + [[sbx(stH, "Y%d_%d" % (l, h), [128, 2, 16, 256 >> l], BF16 if l <= 2 else F32) for h in range(2)]
                           for l in range(1, NL + 1)]
            ct1 = sbx(stH, "ct1", [128, 16, 64]); ct2 = sbx(stH, "ct2", [128, 16, 64])

            def lv(l, h, ri, sl):
                if l == 0:
                    return XH[h][:, ri, :, slice(sl.start + 1, sl.stop + 1, sl.step)]
                return Yl[l][h][:, ri, :, sl]

            def cma(h, l, o_re, o_im, a_re, a_im, b_re, b_im, M):
                for m0 in range(0, M, 64):
                    mm = min(64, M - m0)
                    sl = slice(m0, m0 + mm)
                    dr = bcl(Dl[l][:, 0, 16 * h:16 * h + 16], [128, 16, mm])
                    di = bcl(Dl[l][:, 1, 16 * h:16 * h + 16], [128, 16, mm])
                    t1 = ct1[:, :, 0:mm]; t2 = ct2[:, :, 0:mm]
                    TT('dve', t1, a_re[:, :, sl], dr, ALU.mult)
                    TT('dve', t2, a_im[:, :, sl], di, ALU.mult)
                    TT('dve', t1, t1, t2, ALU.subtract)
                    TT('dve', t2, a_im[:, :, sl], dr, ALU.mult)
                    TT('dve', o_re[:, :, sl], t1, b_re[:, :, sl], ALU.add)
                    TT('dve', t1, a_re[:, :, sl], di, ALU.mult)
                    TT('dve', t1, t1, t2, ALU.add)
                    TT('dve', o_im[:, :, sl], t1, b_im[:, :, sl], ALU.add)

            for h in range(2):
                MSET('dve', XH[h][:, :, :, 0:1], 0.0)
            for l in range(NL):
                M = 256 >> l
                for h in range(2):
                    ev = slice(0, M, 2); od = slice(1, M, 2)
                    cma(h, l, Yl[l + 1][h][:, 0], Yl[l + 1][h][:, 1], lv(l, h, 0, ev), lv(l, h, 1, ev),
                        lv(l, h, 0, od), lv(l, h, 1, od), M // 2)
            MT = 256 >> NL
            YT = Yl[NL]
            sT = [sbx(stH, "sT%d" % h, [128, 2, 16]) for h in range(2)]
            sU = [sbx(stH, "sU%d" % h, [128, 2, 16]) for h in range(2)]
            A1 = [sbx(stH, "A1_%d" % h, [128, 2, 16]) for h in range(2)]
            A2 = [sbx(stH, "A2_%d" % h, [128, 2, 16]) for h in range(2)]
            for h in range(2):
                for ri in range(2):
                    CP('dve', A1[h][:, ri, :], Dl[NL][:, 0, 16 * h:16 * h + 16])
                TS('dve', A2[h][:, 0, :], Dl[NL][:, 1, 16 * h:16 * h + 16], -1.0)
                CP('dve', A2[h][:, 1, :], Dl[NL][:, 1, 16 * h:16 * h + 16])
            for m in range(1, MT):
                for h in range(2):
                    TT('dve', sT[h][:], YT[h][:, :, :, m - 1], A1[h][:], ALU.mult)
                    TT('dve', sU[h][:, 0, :], YT[h][:, 1, :, m - 1], A2[h][:, 0, :], ALU.mult)
                    TT('dve', sU[h][:, 1, :], YT[h][:, 0, :, m - 1], A2[h][:, 1, :], ALU.mult)
                    TT('dve', sT[h][:], sT[h][:], sU[h][:], ALU.add)
                    TT('dve', YT[h][:, :, :, m], YT[h][:, :, :, m], sT[h][:], ALU.add)
            for h in range(2):
                CP('dve', np5[:, :, 16 * h:16 * h + 16], YT[h][:, :, :, MT - 1])
            for l in range(NL - 1, -1, -1):
                M = 256 >> l
                for h in range(2):
                    ev = slice(2, M, 2)
                    pv = slice(0, M // 2 - 1)
                    cma(h, l, lv(l, h, 0, ev), lv(l, h, 1, ev), Yl[l + 1][h][:, 0, :, pv], Yl[l + 1][h][:, 1, :, pv],
                        lv(l, h, 0, ev), lv(l, h, 1, ev), M // 2 - 1)
                    for ri in range(2):
                        CP('dve', lv(l, h, ri, slice(1, M, 2)), Yl[l + 1][h][:, ri, :, :])
        E.barrier()
        dbg('H0', XH[0][:].rearrange("p a b c -> p (a b c)"), [128, 2 * 16 * 257])
        STo(np5_d.ap(), np5[:].rearrange("p a b -> p (a b)"))
        E.pe_fast = 'y' in FAST
        g1 = [sbx(st2, "g1_%d" % i, [128, 512]) for i in range(2)]
        g2 = [sbx(st2, "g2_%d" % i, [128, 512]) for i in range(2)]
        gi = 0
        for sp in (6, 4, 2, 0):
            for o in range(8):
                ps = bank()
                for q in range(2):
                    s1 = sp + q
                    reg = ps[:, q * 256:(q + 1) * 256]
                    for s in range(s1 + 1):
                        MM(reg, lhsT=Kmat[o][:, s1 - s, :], rhs=uT[o][:, s * 256:(s + 1) * 256], start=(s == 0), stop=False)
                    for jj in range(4):
                        pair = 4 * o + jj
                        for ri in range(2):
                            MM(ps[32 * jj:32 * jj + 32, q * 256:(q + 1) * 256], lhsT=WCz[:, pair, s1 + 1, ri, :],
                               rhs=XH[pair // 16][:, ri, pair % 16, 0:256], start=False, stop=(ri == 1),
                               tp=(0, 32 * jj))
                outv = uT[o][:, sp * 256:(sp + 2) * 256].rearrange("p (q k) -> p q k", q=2)
                gelu_evac(outv, ps[:, :].rearrange("p (q k) -> p q k", q=2),
                          g1[gi % 2][:].rearrange("p (q k) -> p q k", q=2), g2[gi % 2][:].rearrange("p (q k) -> p q k", q=2))
                gi += 1
        ps = bank()
        for o in range(8):
            reg = ps[:, o * NS:(o + 1) * NS]
            MM(reg, lhsT=Kmat[o][:, 0, :], rhs=uT[o][:, T:T + NS], start=True, stop=False)
            for jj in range(4):
                pair = 4 * o + jj
                for ri in range(2):
                    MM(ps[32 * jj:32 * jj + 32, o * NS:(o + 1) * NS], lhsT=WCz[:, pair, 1, ri, :],
                       rhs=h0b[:, ri, pair, :], start=False, stop=(ri == 1), tp=(0, 32 * jj))
        gs = sbx(st2, "gs", [128, 8 * NS])
        gelu_evac(gs[:], ps[:, 0:8 * NS], g1[0][:, 0:8 * NS], g2[0][:, 0:8 * NS])
        for o in range(8):
            CP('dve', uT[o][:, T:T + NS], gs[:, o * NS:(o + 1) * NS])
    E.pe_fast = False
    stS5.close()
    E.barrier()
    if 'y5' in debug:
        d = nc.dram_tensor("dbg_y5", [128, 8, T + NS], BF16, kind="ExternalOutput"); dbg_d['y5'] = d
        for o in range(8):
            STo(d.ap()[:, o, :], uT[o][:])

    dq[0] = 'pool'
    E.skip_groups.update([y_p_d.ap().name, y_s_d.ap().name, np_ssdT_d.ap().name])
    pbc = sb("pbc", [128, PB_N]); Abc = sb("Abc", [128, 16]); Dident = sb("Dident", [128, 16, 128], BF16)
    diagw = sb("diagw", [128, 4, 12, 128], BF16)
    ring = [sb("wring%d" % i, [128, 8, 512], BF16) for i in range(3)]
    rgi = [0]
    x_tm = sb("x_tm", [128, 4, 1024]); hnT3 = sb("hnT3", [128, 8, 512], BF16); hfT3 = sb("hfT3", [128, 8, 512], BF16)
    SPLIT[x_tm.name] = (4096, 1024)
    ymixT = sb("ymixT", [128, 16, 512], BF16)
    ssq4 = sb("ssq4", [128, 4]); rs4 = sb("rs4", [128, 4]); tmp4b = sb("tmp4b", [128, 4])
    LD(pbc[:], pbc_d.ap())
    MSET('dve', HT[:], 0.0); MSET('dve', HTb[:], 0.0)
    ACT(Abc[:], pbc[:, PB_ALOG:PB_ALOG + 16], AF.Exp)
    TS('dve', Abc[:], Abc[:], -1.0)
    for h in range(16):
        TS('dve', Dident[:, h, :], ident_f, pbc[:, PB_DSSD + h:PB_DSSD + h + 1])
    for k in range(4):
        for ft in range(12):
            c = PC_CW + k * 12 + ft
            TS('dve', diagw[:, k, ft, :], ident_f, pcol[:, c:c + 1])

    carry = sb("carry", [128, 12, 3], BF16)
    NEGm = sb("NEGm", [128, 4, 128])
    for q in range(4):
        TS('dve', NEGm[:, q, :], tri_f, 30000.0, -30000.0, ALU.mult, ALU.add)
    hnT3_s = sb("hnT3_s", [128, 8, NS], BF16); hfT3_s = sb("hfT3_s", [128, 8, NS], BF16)
    ymixT_s = sb("ymixT_s", [128, 16, NS], BF16)
    ssq4s = sb("ssq4s", [16, 1]); rs4s = sb("rs4s", [16, 1]); tmp4s = sb("tmp4s", [16, 1])
    for blk in range(4):
        S3 = blk == 3
        smp = False
        n = 512
        rows = 128
        ntt = 4
        c0 = blk * 512
        ss_small = ExitStack()
        if S3:
            zT = sbx(ss_small, "zT", [128, 8, NS]); xbs = sbx(ss_small, "xbs", [128, 12, NS]); dts = sbx(ss_small, "dts", [16, NS])
        sc_ = ExitStack()
        zs = sbx(sc_, "zs", [128, 4, 1024], BF16)
        xcT = sbx(sc_, "xcT", [128, 12, 512], BF16)
        SPLIT[zs.name] = (4096, 1024); SPLIT[xcT.name] = (12 * 512, 512)
        dtv = sbx(sc_, "dtv", [128, 4, 16]); dta = sbx(sc_, "dta", [128, 4, 16]); dtl = sbx(sc_, "dtl", [128, 4, 16])
        lastx = sbx(sc_, "lastx", [128, 12, 3])
        sx = ExitStack()
        xbcT = sbx(sx, "xbcT", [128, 12, 515], BF16)
        SPLIT[xbcT.name] = (12 * 515, 515)
        E.pe_fast = 'glu' in FAST
        with ExitStack() as sa:
            o5 = [sbx(sa, "o5_%d" % i, [128, 512]) for i in range(8)]
            sq = [sbx(sa, "sq%d" % i, [128, 512]) for i in range(2)]
            gate = [sbx(sa, "gate%d" % i, [128, 512]) for i in range(2)]
            r5 = sbx(sa, "r5", [128, 512]); r5t = sbx(sa, "r5t", [128, 512])

            def ublk(kt, smp=smp, blk=blk):
                if smp:
                    return uT[kt][:, T:T + NS]
                return uT[kt][:, 0:T].rearrange("p (s k) -> p s k", s=8)[:, :, 64 * blk:64 * blk + 64]

            def psv(t, smp=smp):
                if smp:
                    return t[:, 0:NS]
                return t[:, 0:512].rearrange("p (s k) -> p s k", s=8)
            wg = [wunit(ring, rgi, w_glu_d, 0, 0, bf=wb_glu), wunit(ring, rgi, w_glu_d, 0, 512, bf=wb_glu)]
            pss = bank()
            if S3:
                o5s = sbx(sa, "o5s", [128, 8, NS]); sqs_ = [sbx(sa, "sqs%d" % i, [128, NS]) for i in range(2)]
                gts = [sbx(sa, "gts%d" % i, [128, NS]) for i in range(2)]
                r5s = sbx(sa, "r5s", [128, NS]); r5ts = sbx(sa, "r5ts", [128, NS])
                pss_s = bank()
            for oc in range(8):
                ps = bank(avoid=pss)
                if S3 and ps.name == pss_s.name:
                    ps = bank(avoid=pss)
                for kt in range(8):
                    MM(psv(ps), lhsT=wg[oc // 4][:, kt, (oc % 4) * 128:(oc % 4) * 128 + 128], rhs=ublk(kt),
                       start=(kt == 0), stop=(kt == 7))
                ACT(gate[oc % 2][:, 0:n], ps[:, 0:n], AF.Sigmoid, bias=pcol[:, PC_BGLU + oc:PC_BGLU + oc + 1])
                TT('dve', psv(o5[oc]), psv(gate[oc % 2]), ublk(oc), ALU.mult)
                ACT(sq[oc % 2][:, 0:n], o5[oc][:, 0:n], AF.Square)
                MM(pss[:, 0:n], lhsT=ones_f[:, :], rhs=sq[oc % 2][:, 0:n], start=(oc == 0), stop=(oc == 7))
            if S3:
                for oc in range(8):
                    ps2 = bank(avoid=pss)
                    if ps2.name == pss_s.name:
                        ps2 = bank(avoid=pss)
                    for kt in range(8):
                        MM(ps2[:, 0:NS], lhsT=wg[oc // 4][:, kt, (oc % 4) * 128:(oc % 4) * 128 + 128], rhs=uT[kt][:, T:T + NS],
                           start=(kt == 0), stop=(kt == 7))
                    ACT(gts[oc % 2][:], ps2[:, 0:NS], AF.Sigmoid, bias=pcol[:, PC_BGLU + oc:PC_BGLU + oc + 1])
                    TT('dve', o5s[:, oc, :], gts[oc % 2][:], uT[oc][:, T:T + NS], ALU.mult)
                    ACT(sqs_[oc % 2][:], o5s[:, oc, :], AF.Square)
                    MM(pss_s[:, 0:NS], lhsT=ones_f[:, :], rhs=sqs_[oc % 2][:], start=(oc == 0), stop=(oc == 7))
            rstd_from(r5[:, 0:n], pss[:, 0:n], 1024.0, r5t[:, 0:n])
            for oc in range(8):
                if smp:
                    STT(ymixT[:, oc, 0:n], o5[oc][:, 0:n], pcol[:, PC_S5N + oc:PC_S5N + oc + 1], r5[:, 0:n], ALU.mult, ALU.mult)
                else:
                    STT(ymixT[:, oc, :].rearrange("p (k s) -> p s k", s=8), psv(o5[oc]), pcol[:, PC_S5N + oc:PC_S5N + oc + 1],
                        psv(r5), ALU.mult, ALU.mult)
            if S3:
                rstd_from(r5s[:], pss_s[:, 0:NS], 1024.0, r5ts[:])
                for oc in range(8):
                    STT(ymixT_s[:, oc, :], o5s[:, oc, :], pcol[:, PC_S5N + oc:PC_S5N + oc + 1], r5s[:], ALU.mult, ALU.mult)
        if blk == 0:
            dbg('y5n', ymixT[:, 0:8, :], [128, 8, 512])
        E.pe_fast = 'inproj' in FAST
        if smp:
            LD(x_tm[0:NS, 0, :], xs_tm.ap())
        else:
            LD(x_tm[:], xp_tm.ap()[blk * 512:(blk + 1) * 512, :].rearrange("(tt p) f -> p tt f", p=128))
        if blk == 0:
            load_hnT(xp_T, 0, 512, hnT3, rstdbc_all[:, 0:512])
        rbc = rstdbc_s[:, :] if smp else rstdbc_all[:, blk * 512:(blk + 1) * 512]
        if not smp:
            wz = [wunit(ring, rgi, w_in_d, 0, 1024, bf=wb_in), wunit(ring, rgi, w_in_d, 0, 1536, bf=wb_in)]
            for half in range(2):
                for tt in range(4):
                    ps = bank()
                    for kt in range(8):
                        MM(ps[:, :], lhsT=hnT3[:, kt, tt * 128:(tt + 1) * 128], rhs=wz[half][:, kt, :], start=(kt == 0), stop=(kt == 7))
                    ACT(zs[:, tt, half * 512:(half + 1) * 512], ps[:, :], AF.Silu)
                if S3:
                    for oc in range(4 * half, 4 * half + 4):
                        ps = bank()
                        for kt in range(8):
                            MM(ps[:, 0:NS], lhsT=wz[half][:, kt, (oc % 4) * 128:(oc % 4 + 1) * 128], rhs=hnT3_s[:, kt, :], start=(kt == 0), stop=(kt == 7))
                        ACT(zT[:, oc, :], ps[:, 0:NS], AF.Silu)
            if blk == 0:
                MSET('dve', xbcT[:, :, 0:3], 0.0)
            else:
                CP('dve', xbcT[:, :, 0:3], carry[:])
            for un in range(3):
                wx = wunit(ring, rgi, w_in_d, 0, 2048 + 512 * un, bf=wb_in)
                for sub in range(4):
                    ft = 4 * un + sub
                    ps = bank()
                    for kt in range(8):
                        MM(ps[:, :], lhsT=wx[:, kt, sub * 128:(sub + 1) * 128], rhs=hnT3[:, kt, :], start=(kt == 0), stop=(kt == 7))
                    CP('dve', xbcT[:, ft, 3:515], ps[:, :])
                    if blk == 3:
                        CP('dve', lastx[:, ft, :], ps[:, 509:512])
                    if S3:
                        ps = bank()
                        for kt in range(8):
                            MM(ps[:, 0:NS], lhsT=wx[:, kt, sub * 128:(sub + 1) * 128], rhs=hnT3_s[:, kt, :], start=(kt == 0), stop=(kt == 7))
                        CP('dve', xbs[:, ft, :], ps[:, 0:NS])
            CP('dve', carry[:], xbcT[:, :, 512:515])
            if blk == 3:
                STo(np_convT_d.ap(), lastx[:].rearrange("p a b -> p (a b)"))
            wd = wunit(ring, rgi, w_in_d, 0, 3584, 16, bf=wb_in)
            ps = bank()
            for tt in range(4):
                for kt in range(8):
                    MM(ps[:, tt * 16:(tt + 1) * 16], lhsT=hnT3[:, kt, tt * 128:(tt + 1) * 128], rhs=wd[:, kt, 0:16], start=(kt == 0), stop=(kt == 7))
                TT('dve', dtv[:, tt, :], ps[:, tt * 16:(tt + 1) * 16], pbc[:, PB_DTB:PB_DTB + 16], ALU.add)
            if S3:
                ps = bank()
                for kt in range(8):
                    MM(ps[0:16, 0:NS], lhsT=wd[:, kt, 0:16], rhs=hnT3_s[:, kt, :], start=(kt == 0), stop=(kt == 7))
                CP('dve', dts[:], ps[0:16, 0:NS])
            TS('dve', dtl[:], dtv[:], -1.0)
            TT('dve', dtl[:], dtl[:], dtv[:], ALU.min)
            ACT(dtl[:], dtl[:], AF.Exp)
            TS('dve', dtl[:], dtl[:], 1.0, None, ALU.add)
            ACT(dtl[:], dtl[:], AF.Ln)
            STT(dtv[:], dtv[:], 0.0, dtl[:], ALU.max, ALU.add)
            TT('dve', dta[:], dtv[:], bcm(Abc[:], [128, 4, 16]), ALU.mult)
            for ft in range(12):
                ps = bank()
                for k in range(4):
                    MM(ps[:, :], lhsT=diagw[:, k, ft, :], rhs=xbcT[:, ft, k:k + 512], start=(k == 0), stop=(k == 3))
                ACT(xcT[:, ft, :], ps[:, :], AF.Silu, bias=pcol[:, PC_CB + ft:PC_CB + ft + 1])
            if blk == 0:
                dbg('xc', xcT[:], [128, 12, 512], sc_)
            sx.close()
            E.barrier(exclude='sp')
            E.pe_fast = 'ssd' in FAST
            x_t = [sbx(sc_, "x_t%d" % i, [128, 1024], BF16) for i in range(2)]
            B_t = [sbx(sc_, "B_t%d" % i, [128, 256], BF16) for i in range(2)]
            acum = [sbx(sc_, "acum%d" % i, [128, 16]) for i in range(2)]
            nacum = [sbx(sc_, "nacum%d" % i, [128, 16]) for i in range(2)]
            cdb = [sbx(sc_, "cdb%d" % i, [128, 16]) for i in range(2)]
            dsd = [sbx(sc_, "dsd%d" % i, [128, 16]) for i in range(2)]
            ea = [sbx(sc_, "ea%d" % i, [128, 16]) for i in range(2)]
            xdt = [sbx(sc_, "xdt%d" % i, [128, 1024], BF16) for i in range(2)]
            xsd = [sbx(sc_, "xsd%d" % i, [128, 1024], BF16) for i in range(2)]
            CBm = [sbx(sc_, "CBm%d" % i, [128, 2, 128], BF16) for i in range(2)]
            Em = [sbx(sc_, "Em%d" % i, [128, 16, 128], BF16) for i in range(2)]
            diagA = sbx(sc_, "diagA", [128, 16, 128])
            yv = sbx(sc_, "yv", [128, 1024]); yv2 = sbx(sc_, "yv2", [128, 1024]); ynb = sbx(sc_, "ynb", [128, 1024], BF16)

            def ssd_front(c, i):
                cs = slice(c * 128, (c + 1) * 128)
                pst = bank()
                pstb = pst[:, :].bitcast(BF16)
                for ft in range(8):
                    TR(pstb[:, ft * 128:(ft + 1) * 128], xcT[:, ft, cs], ident_b[:])
                CP('dve', x_t[i][:], pstb)
                pst2 = bank()
                pst2b = pst2[:, :].bitcast(BF16)
                for g in range(2):
                    TR(pst2b[:, g * 128:(g + 1) * 128], xcT[:, 8 + g, cs], ident_b[:])
                CP('act', B_t[i][:], pst2b[:, 0:256])
                psa = bank()
                MM(psa[:, 0:16], lhsT=tri_f, rhs=dta[:, c, :], start=True, stop=True)
                MM(psa[:, 16:32], lhsT=ones_f[:, :], rhs=dta[:, c, :], start=True, stop=True)
                CP('dve', acum[i][:], psa[:, 0:16])
                TS('dve', nacum[i][:], psa[:, 0:16], -1.0)
                ACT(cdb[i][:], psa[:, 16:32], AF.Exp)
                TT('dve', dsd[i][:], psa[:, 16:32], acum[i][:], ALU.subtract)
                ACT(dsd[i][:], dsd[i][:], AF.Exp)
                ACT(ea[i][:], acum[i][:], AF.Exp)
                TT('dve', xdt[i][:].rearrange("p (h q) -> p h q", q=64), x_t[i][:].rearrange("p (h q) -> p h q", q=64),
                   bcl(dtv[:, c, :], [128, 16, 64]), ALU.mult)
                TT('dve', xsd[i][:].rearrange("p (h q) -> p h q", q=64), xdt[i][:].rearrange("p (h q) -> p h q", q=64),
                   bcl(dsd[i][:], [128, 16, 64]), ALU.mult)
                psc = bank()
                for g in range(2):
                    MM(psc[:, g * 128:(g + 1) * 128], lhsT=xcT[:, 8 + g, cs], rhs=xcT[:, 10 + g, cs], start=True, stop=True)
                CP('dve', CBm[i][:], psc[:, 0:256].rearrange("p (g l) -> p g l", g=2))
                TT('dve', diagA[:], bcm(ident_f, [128, 16, 128]), bcl(acum[i][:], [128, 16, 128]), ALU.mult)
                pse = [bank() for _ in range(4)]
                for q in range(4):
                    MM(pse[q][:, :], lhsT=ones_f[:, :], rhs=diagA[:, 4 * q:4 * q + 4, :], start=True, stop=False)
                    MM(pse[q][:, :], lhsT=ident_f, rhs=NEGm[:, :, :], start=False, stop=True)
                for h in range(16):
                    ACT(Em[i][:, h, :], pse[h // 4][:, (h % 4) * 128:(h % 4 + 1) * 128], AF.Exp, bias=nacum[i][:, h:h + 1])

            def ssd_back(c, i):
                cs = slice(c * 128, (c + 1) * 128)
                for g in range(2):
                    TT('dve', Em[i][:, 8 * g:8 * g + 8, :], Em[i][:, 8 * g:8 * g + 8, :],
                       CBm[i][:, g:g + 1, :].to_broadcast([128, 8, 128]), ALU.mult)
                psy = [bank(), bank()]
                for h in range(16):
                    reg = psy[h // 8][:, (h % 8) * 64:(h % 8 + 1) * 64]
                    MM(reg, lhsT=Em[i][:, h, :], rhs=xdt[i][:, h * 64:(h + 1) * 64], start=True, stop=False)
                    MM(reg, lhsT=Dident[:, h, :], rhs=x_t[i][:, h * 64:(h + 1) * 64], start=False, stop=True)
                pso = [bank(), bank()]
                for g in range(2):
                    MM(pso[g][:, :], lhsT=xcT[:, 10 + g, cs], rhs=HTb[:, g * 512:(g + 1) * 512], start=True, stop=True)
                for g in range(2):
                    TT('dve', yv[:, g * 512:(g + 1) * 512].rearrange("p (h q) -> p h q", q=64),
                       pso[g][:, :].rearrange("p (h q) -> p h q", q=64), bcl(ea[i][:, 8 * g:8 * g + 8], [128, 8, 64]), ALU.mult)
                    TT('dve', yv[:, g * 512:(g + 1) * 512], yv[:, g * 512:(g + 1) * 512], psy[g][:, :], ALU.add)
                pss2 = [bank(), bank()]
                for g in range(2):
                    MM(pss2[g][:, :], lhsT=B_t[i][:, g * 128:(g + 1) * 128], rhs=xsd[i][:, g * 512:(g + 1) * 512], start=True, stop=True)
                TT('dve', HT[:].rearrange("p (h q) -> p h q", q=64), HT[:].rearrange("p (h q) -> p h q", q=64),
                   bcl(cdb[i][:], [128, 16, 64]), ALU.mult)
                for g in range(2):
                    TT('dve', HT[:, g * 512:(g + 1) * 512], HT[:, g * 512:(g + 1) * 512], pss2[g][:, :], ALU.add)
                CP('act', HTb[:], HT[:])
                TT('dve', yv[:], yv[:], zs[:, c, :], ALU.mult)
                for g in range(2):
                    ACT(yv2[:, g * 512:(g + 1) * 512], yv[:, g * 512:(g + 1) * 512], AF.Square, accum_out=ssq4[:, g:g + 1])
                rstd_from(rs4[:, 0:2], ssq4[:, 0:2], 512.0, tmp4b[:, 0:2])
                for g in range(2):
                    TS('dve', ynb[:, g * 512:(g + 1) * 512], yv[:, g * 512:(g + 1) * 512], rs4[:, g:g + 1])
                pst3 = bank()
                pst3b = pst3[:, :].bitcast(BF16)
                for ft in range(8):
                    TR(pst3b[:, ft * 128:(ft + 1) * 128], ynb[:, ft * 128:(ft + 1) * 128], ident_b[:])
                TT('dve', ymixT[:, 8:16, cs], pst3b.rearrange("p (a b) -> p a b", b=128),
                   bcl(pcol[:, PC_SSDN:PC_SSDN + 8], [128, 8, 128]), ALU.mult)

            ssd_front(0, 0)
            for c in range(4):
                if c < 3:
                    ssd_front(c + 1, (c + 1) % 2)
                ssd_back(c, c % 2)
            if blk == 3:
                STo(np_ssdT_d.ap(), HT[:])
            if blk == 0:
                dbg('yssdT', ymixT[:, 8:16, :], [128, 8, 512], sc_)
            sc_.close()
            E.barrier(exclude='sp')
            if S3:
                E.pe_fast = 'smp' in FAST
                sc_ = ExitStack()
                p16 = sbx(sc_, "p16", [16, 1027])
                LD(p16[:], p16_d.ap())
                c0T = sbx(sc_, "c0T", [128, 12, 3, NS])
                acc = sbx(sc_, "acc", [128, 12, NS]); tcv = sbx(sc_, "tcv", [128, 12, NS]); xcs = sbx(sc_, "xcs", [128, 12, NS])
                dtls = sbx(sc_, "dtls", [16, NS]); decs = sbx(sc_, "decs", [16, NS]); Acol = sbx(sc_, "Acol", [16, 1])
                dtE = sbx(sc_, "dtE", [128, 8, NS]); decE = sbx(sc_, "decE", [128, 8, NS]); dtx = sbx(sc_, "dtx", [128, 8, NS])
                Dcol = sbx(sc_, "Dcol", [128, 8]); yvs = sbx(sc_, "yvs", [128, 8, NS]); y2s = sbx(sc_, "y2s", [128, 8, NS])
                sqs = sbx(sc_, "sqs", [128, 8, NS]); rgs = sbx(sc_, "rgs", [128, 2, NS]); rgt = sbx(sc_, "rgt", [128, 2, NS])
                dgb = [sbx(sc_, "dgb%d" % i, [128, 128]) for i in range(4)]
                h0s = [sbx(sc_, "h0s%d" % i, [128, 8, 128]) for i in range(2)]
                hns = [sbx(sc_, "hns%d" % i, [128, 8, 128]) for i in range(2)]
                prs = sbx(sc_, "prs", [128, 8, 128])
                for t_ in h0s + hns:
                    SPLIT[t_.name] = (1024, 128)
                LD(c0T[:].rearrange("p a b c -> p (a b c)"), conv0T_d.ap())
                E.dma(dq[0], ns_conv_a_d.ap().rearrange("b (k c) -> b k c", k=2), conv0_d.ap().rearrange("b (k c) -> b k c", k=3)[:, 1:3, :],
                      writes=[ns_conv_a_d.ap()])
                STo(ns_conv_bT_d.ap(), xbs[:].rearrange("p a b -> p (a b)"))
                TS('dve', dts[:], dts[:], p16[:, 0:1], None, ALU.add)
                TS('dve', dtls[:], dts[:], -1.0)
                TT('dve', dtls[:], dtls[:], dts[:], ALU.min)
                ACT(dtls[:], dtls[:], AF.Exp)
                TS('dve', dtls[:], dtls[:], 1.0, None, ALU.add)
                ACT(dtls[:], dtls[:], AF.Ln)
                STT(dts[:], dts[:], 0.0, dtls[:], ALU.max, ALU.add)
                ACT(Acol[:], p16[:, 1:2], AF.Exp)
                TS('dve', Acol[:], Acol[:], -1.0)
                TS('dve', dtls[:], dts[:], Acol[:, 0:1])
                ACT(decs[:], dtls[:], AF.Exp)
                cwv = pcol[:, PC_CW:PC_CW + 48].rearrange("p (k f) -> p k f", k=4)
                TT('dve', acc[:], c0T[:, :, 0, :], bcl(cwv[:, 0, :], [128, 12, NS]), ALU.mult)
                for k in (1, 2):
                    TT('dve', tcv[:], c0T[:, :, k, :], bcl(cwv[:, k, :], [128, 12, NS]), ALU.mult)
                    TT('dve', acc[:], acc[:], tcv[:], ALU.add)
                TT('dve', tcv[:], xbs[:], bcl(cwv[:, 3, :], [128, 12, NS]), ALU.mult)
                TT('dve', acc[:], acc[:], tcv[:], ALU.add)
                TT('dve', acc[:], acc[:], bcl(pcol[:, PC_CB:PC_CB + 12], [128, 12, NS]), ALU.add)
                ACT(xcs[:], acc[:], AF.Silu)
                pse = bank()
                for hp in range(8):
                    e16 = p16[:, 3 + hp * 128:3 + (hp + 1) * 128]
                    MM(pse[:, hp * NS:(hp + 1) * NS], lhsT=e16, rhs=dts[:], start=True, stop=True)
                    MM(pse[:, 128 + hp * NS:128 + (hp + 1) * NS], lhsT=e16, rhs=decs[:], start=True, stop=True)
                    MM(pse[:, 256 + hp:256 + hp + 1], lhsT=e16, rhs=p16[:, 2:3], start=True, stop=True)
                CP('dve', dtE[:], pse[:, 0:128].rearrange("p (a b) -> p a b", b=NS))
                CP('dve', decE[:], pse[:, 128:256].rearrange("p (a b) -> p a b", b=NS))
                CP('dve', Dcol[:], pse[:, 256:264])
                TT('dve', dtx[:], xcs[:, 0:8, :], dtE[:], ALU.mult)
                def bc_rows(b):
                    pb_ = bank()
                    for i in range(4):
                        TS('dve', dgb[i][:], ident_f, xcs[:, 8 + i, b:b + 1])
                        MM(pb_[:, i * 128:(i + 1) * 128], lhsT=ones_f[:, :], rhs=dgb[i][:], start=True, stop=True)
                    return pb_

                LD(h0s[0][:], ssd0_d.ap()[0].rearrange("(hp q) n -> q hp n", q=128))
                psb_next = bc_rows(0)
                for b in range(NS):
                    h0t = h0s[b % 2]; hnt = hns[b % 2]
                    psb = psb_next
                    if b + 1 < NS:
                        LD(h0s[(b + 1) % 2][:], ssd0_d.ap()[b + 1].rearrange("(hp q) n -> q hp n", q=128))
                        psb_next = bc_rows(b + 1)
                    for hp in range(8):
                        g = hp // 4
                        ACT(h0t[:, hp, :], h0t[:, hp, :], AF.Copy, scale=decE[:, hp, b:b + 1])
                        STT(hnt[:, hp, :], psb[:, g * 128:(g + 1) * 128], dtx[:, hp, b:b + 1], h0t[:, hp, :], ALU.mult, ALU.add)
                    STo(ns_ssd_d.ap()[b].rearrange("(hp q) n -> q hp n", q=128), hnt[:])
                    for g in range(2):
                        TT('dve', prs[:, 4 * g:4 * g + 4, :], hnt[:, 4 * g:4 * g + 4, :],
                           bcm(psb[:, (2 + g) * 128:(3 + g) * 128], [128, 4, 128]), ALU.mult)
                    RED(yvs[:, :, b], prs[:])
                TT('dve', y2s[:], xcs[:, 0:8, :], bcl(Dcol[:], [128, 8, NS]), ALU.mult)
                TT('dve', y2s[:], y2s[:], yvs[:], ALU.add)
                TT('dve', y2s[:], y2s[:], zT[:], ALU.mult)
                ACT(sqs[:], y2s[:], AF.Square)
                psg = bank()
                for g in range(2):
                    for i in range(4):
                        MM(psg[:, g * NS:(g + 1) * NS], lhsT=ones_f[:, :], rhs=sqs[:, 4 * g + i, :], start=(i == 0), stop=(i == 3))
                rstd_from(rgs[:].rearrange("p a b -> p (a b)"), psg[:, 0:2 * NS], 512.0, rgt[:].rearrange("p a b -> p (a b)"))
                for g in range(2):
                    TT('dve', y2s[:, 4 * g:4 * g + 4, :], y2s[:, 4 * g:4 * g + 4, :], rgs[:, g:g + 1, :].to_broadcast([128, 4, NS]), ALU.mult)
                TT('dve', ymixT_s[:, 8:16, :], y2s[:], bcl(pcol[:, PC_SSDN:PC_SSDN + 8], [128, 8, NS]), ALU.mult)
                sc_.close()
                E.barrier(exclude='sp')
            ss_small.close()
        if blk < 3:
            load_hnT(xp_T, (blk + 1) * 512, 512, hnT3, rstdbc_all[:, (blk + 1) * 512:(blk + 2) * 512])
        if blk == 2:
            load_hnT(xs_T, 0, NS, hnT3_s, rstdbc_s[:, :])
        E.pe_fast = 'dense' in FAST
        sxs = ExitStack()
        TL = [dict(x=x_tm[:, tt, :], rows=128, ym=ymixT, hf=hfT3, c0=tt * 128, smp=False, tt=tt) for tt in range(4)]
        if S3:
            x_tm_s = sbx(sxs, "x_tm_s", [NS, 1024])
            LD(x_tm_s[:], xs_tm.ap())
            TL.append(dict(x=x_tm_s[:, :], rows=NS, ym=ymixT_s, hf=hfT3_s, c0=0, smp=True, tt=0))
        for half in range(2):
            pss_ = [bank() for _ in TL]
            for kh in range(2):
                wo = wunit(ring, rgi, w_out_d, kh * 1024, half * 512, bf=wb_out)
                for ti, tl in enumerate(TL):
                    r = tl['rows']
                    for kt in range(8):
                        MM(pss_[ti][0:r, :], lhsT=tl['ym'][:, kh * 8 + kt, tl['c0']:tl['c0'] + r], rhs=wo[:, kt, :],
                           start=(kh == 0 and kt == 0), stop=(kh == 1 and kt == 7))
            for ti, tl in enumerate(TL):
                r = tl['rows']
                xs_ = tl['x'][0:r, half * 512:(half + 1) * 512]
                TT('dve', xs_, xs_, pss_[ti][0:r, :], ALU.add)
        if blk == 0:
            dbg('x1', x_tm[:], [128, 4, 1024], None, F32)
        with ExitStack() as se:
            hfb = sbx(se, "hfb", [128, 4, 1024], BF16)
            hidT = sbx(se, "hidT", [128, 32, 512], BF16)
            SPLIT[hfb.name] = (4096, 1024); SPLIT[hidT.name] = (32 * 512, 512)
            rl = [sbx(se, "rl%d" % i, [128, 512], BF16) for i in range(2)]
            junk3 = sbx(se, "junk3", [128, 1024])
            if S3:
                hfb_s = sbx(se, "hfb_s", [NS, 1024], BF16); hidT_s = sbx(se, "hidT_s", [128, 32, NS], BF16)
                rl_s = [sbx(se, "rl_s%d" % i, [128, NS], BF16) for i in range(2)]

            def norm_stats(tl):
                r = tl['rows']
                if tl['smp']:
                    sq_, rs_, tm_ = ssq4s[:, 0:1], rs4s[:, 0:1], tmp4s[:, 0:1]
                else:
                    tt = tl['tt']
                    sq_, rs_, tm_ = ssq4[:, tt:tt + 1], rs4[:, tt:tt + 1], tmp4b[:, tt:tt + 1]
                ACT(junk3[0:r, :], tl['x'][0:r, :], AF.Square, accum_out=sq_)
                rstd_from(rs_, sq_, 1024.0, tm_)
                return rs_

            for tl in TL:
                r = tl['rows']
                rs_ = norm_stats(tl)
                hb = hfb_s[:, :] if tl['smp'] else hfb[:, tl['tt'], :]
                TS('dve', hb[0:r, :], tl['x'][0:r, :], rs_)
                pst = bank()
                pstb = pst[:, :].bitcast(BF16)
                for kt in range(8):
                    TR(pstb[:, kt * 128:kt * 128 + r], hb[0:r, kt * 128:(kt + 1) * 128], ident_b[0:r, 0:r])
                TT('dve', tl['hf'][:, :, tl['c0']:tl['c0'] + r], pstb.rearrange("p (a b) -> p a b", b=128)[:, :, 0:r],
                   bcl(pcol[:, PC_FFN:PC_FFN + 8], [128, 8, r]), ALU.mult)
            for un in range(8):
                w1 = wunit(ring, rgi, w_ff1_d, 0, 512 * un, bf=wb_ff1)
                for sub in range(4):
                    ft = 4 * un + sub
                    ps = bank()
                    for kt in range(8):
                        MM(ps[:, :], lhsT=w1[:, kt, sub * 128:(sub + 1) * 128], rhs=hfT3[:, kt, :], start=(kt == 0), stop=(kt == 7))
                    ACT(rl[ft % 2][:, :], ps[:, :], AF.Relu)
                    TT('dve', hidT[:, ft, :], rl[ft % 2][:, :], rl[ft % 2][:, :], ALU.mult)
                    if S3:
                        ps = bank()
                        for kt in range(8):
                            MM(ps[:, 0:NS], lhsT=w1[:, kt, sub * 128:(sub + 1) * 128], rhs=hfT3_s[:, kt, :], start=(kt == 0), stop=(kt == 7))
                        ACT(rl_s[ft % 2][:, :], ps[:, 0:NS], AF.Relu)
                        TT('dve', hidT_s[:, ft, :], rl_s[ft % 2][:, :], rl_s[ft % 2][:, :], ALU.mult)
            for half in range(2):
                pss_ = [bank() for _ in TL]
                for q in range(4):
                    w2 = wunit(ring, rgi, w_ff2_d, q * 1024, half * 512, bf=wb_ff2)
                    for ti, tl in enumerate(TL):
                        r = tl['rows']
                        hd = hidT_s if tl['smp'] else hidT
                        for kt in range(8):
                            MM(pss_[ti][0:r, :], lhsT=hd[:, q * 8 + kt, tl['c0']:tl['c0'] + r], rhs=w2[:, kt, :],
                               start=(q == 0 and kt == 0), stop=(q == 3 and kt == 7))
                for ti, tl in enumerate(TL):
                    r = tl['rows']
                    xs_ = tl['x'][0:r, half * 512:(half + 1) * 512]
                    TT('dve', xs_, xs_, pss_[ti][0:r, :], ALU.add)
            E.pe_fast = False
            for tl in TL:
                r = tl['rows']
                rs_ = norm_stats(tl)
                STT(tl['x'][0:r, :], tl['x'][0:r, :], rs_, pbc[0:r, PB_FIN:PB_FIN + 1024], ALU.mult, ALU.mult)
            STo(y_p_d.ap()[blk * 512:(blk + 1) * 512, :].rearrange("(tt p) f -> p tt f", p=128), x_tm[:])
            if S3:
                STo(y_s_d.ap(), x_tm_s[:, :])
        E.barrier(exclude='sp')
        sxs.close()

    E.finish('sp')
    E.build()
    st.close()
    return nc, dbg_d, E


def _l1(a):
    sh = a.shape[2:]
    a = a.reshape((32, 2, 64) + sh)
    perm = (1, 2, 0) + tuple(range(3, 3 + len(sh)))
    return np.ascontiguousarray(a.transpose(perm)).reshape((128, 32) + sh)


def _host_inputs(inp, c):
    f = np.float32
    g = lambda k: np.asarray(inp[k], dtype=f)
    xp = g("x_prompt")[c]
    xs = g("x_sample")[16 * c:16 * c + 16, 0]
    m = {}
    m["xp_tm"] = np.ascontiguousarray(xp); m["xp_T"] = np.ascontiguousarray(xp.T)
    m["xs_tm"] = np.ascontiguousarray(xs); m["xs_T"] = np.ascontiguousarray(xs.T)
    col = lambda v: np.ascontiguousarray(v.reshape(-1, 128).T)
    cw = g("ssd_conv_w")[0]
    pcol = np.concatenate([col(g("norm_mix_w")[0]), col(g("s5_norm_w")[0]), col(g("s5_b_glu")[0]), col(g("s5_d")[0]),
                           col(g("ssd_norm_w")[0]),
                           np.ascontiguousarray(cw.reshape(4, 12, 128).transpose(2, 0, 1)).reshape(128, 48),
                           col(g("ssd_conv_b")[0]), col(g("norm_ffn_w")[0]),
                           np.broadcast_to(np.arange(9, dtype=f).reshape(1, 9), (128, 9))], axis=1)
    m["pcol"] = np.ascontiguousarray(pcol)
    bc = lambda v: np.broadcast_to(v.reshape(1, -1), (128, v.size))
    m["pbc"] = np.ascontiguousarray(np.concatenate([bc(g("norm_final_w")),
                                                    bc(g("ssd_dt_bias")[0]), bc(g("ssd_a_log")[0]), bc(g("ssd_d")[0])], axis=1))
    e16 = np.zeros((16, 1024), f)
    for h in range(16):
        e16[h, h * 64:(h + 1) * 64] = 1.0
    m["p16"] = np.ascontiguousarray(np.concatenate([g("ssd_dt_bias")[0].reshape(16, 1), g("ssd_a_log")[0].reshape(16, 1),
                                                    g("ssd_d")[0].reshape(16, 1), e16], axis=1))
    tri = np.triu(np.ones((128, 128), f))
    m["cst"] = np.ascontiguousarray(np.concatenate([np.eye(128, dtype=f), tri], axis=1))
    lre, lim, ldt = g("s5_lam_re")[0], g("s5_lam_im")[0], g("s5_log_dt")[0]
    bre, bim, cre, cim = g("s5_b_re")[0], g("s5_b_im")[0], g("s5_c_re")[0], g("s5_c_im")[0]

    def cz(cc):
        t = _l1(np.ascontiguousarray(cc.transpose(0, 2, 1)))
        out = np.zeros((128, 32, 2, 16), f)
        out[0:64, :, 0, :] = t[0:64]
        out[64:128, :, 1, :] = t[64:128]
        return out.reshape(128, 1024)

    def bz(bb):
        t = _l1(bb)
        out = np.zeros((128, 32, 2, 16), f)
        out[0:64, :, 0, :] = t[0:64]
        out[64:128, :, 1, :] = t[64:128]
        return out.reshape(128, 1024)

    m["l1pack"] = np.ascontiguousarray(np.concatenate([_l1(lre), _l1(lim), _l1(np.repeat(ldt[:, None], 64, 1)), cz(cre), cz(cim), bz(bre), bz(bim)], axis=1))
    m["gpack"] = np.ascontiguousarray(np.concatenate([lre, lim, ldt.reshape(64, 1)], axis=1))

    def b2(bb):
        out = np.zeros((8, 16, 8, 2, 64), f)
        t = bb.reshape(8, 8, 64, 16)
        for g8 in range(8):
            out[g8, :, :, g8 % 2, :] = t[:, g8].transpose(2, 0, 1)
        return out.reshape(128, 1024)

    m["b2pack"] = np.ascontiguousarray(np.concatenate([b2(bre), b2(bim)], axis=1))
    h0r = g("state_s5_re")[0, 16 * c:16 * c + 16]
    h0i = g("state_s5_im")[0, 16 * c:16 * c + 16]
    h0 = np.stack([_l1(np.ascontiguousarray(h0r.transpose(1, 2, 0))), _l1(np.ascontiguousarray(h0i.transpose(1, 2, 0)))], axis=1)
    m["h0l1"] = np.ascontiguousarray(h0.reshape(128, 1024))
    m["ssd0"] = np.ascontiguousarray(g("state_ssd")[0, 16 * c:16 * c + 16].reshape(16, 1024, 128))
    cv = g("state_conv")[0, 16 * c:16 * c + 16]
    m["conv0"] = np.ascontiguousarray(cv.reshape(16, 3 * 1536))
    m["conv0T"] = np.ascontiguousarray(cv.reshape(16, 3, 12, 128).transpose(3, 2, 1, 0)).reshape(128, 12 * 3 * 16)
    m["w_in"] = g("w_in")[0]; m["w_glu"] = g("s5_w_glu")[0]; m["w_out"] = g("w_out")[0]
    m["w_ff1"] = g("w_ff1")[0]; m["w_ff2"] = g("w_ff2")[0]
    return m


def _unl1(a):
    sh = a.shape[2:]
    a = a.reshape((2, 64, 32) + sh)
    perm = (2, 0, 1) + tuple(range(3, 3 + len(sh)))
    return np.ascontiguousarray(a.transpose(perm)).reshape((64, 64) + sh)


_CACHE = {}


def kernel(**inputs):
    if "nc" not in _CACHE:
        _CACHE["nc"] = build_program()
    nc, _, _ = _CACHE["nc"]
    shared = None
    in_maps = []
    for c in range(8):
        in_maps.append(_host_inputs(inputs, c))
    res = run_bass_kernel_spmd(nc, in_maps, core_ids=list(range(8)))
    R = res.results
    f = np.float32
    y_p = np.stack([R[c]["y_p"] for c in range(8)]).astype(f)
    y_s = np.concatenate([R[c]["y_s"] for c in range(8)]).reshape(128, 1, 1024).astype(f)
    np_re = np.stack([_unl1(R[c]["np5"].reshape(128, 2, 32)[:, 0, :]) for c in range(8)])[None].astype(f)
    np_im = np.stack([_unl1(R[c]["np5"].reshape(128, 2, 32)[:, 1, :]) for c in range(8)])[None].astype(f)
    np_ssd = np.stack([R[c]["np_ssdT"].T.reshape(16, 64, 128) for c in range(8)])[None].astype(f)
    np_conv = np.stack([R[c]["np_convT"].reshape(128, 12, 3).transpose(2, 1, 0).reshape(3, 1536) for c in range(8)])[None].astype(f)
    ns_re = np.concatenate([_unl1(R[c]["ns5"].reshape(128, 2, 32, 16)[:, 0]).transpose(2, 0, 1) for c in range(8)])[None].astype(f)
    ns_im = np.concatenate([_unl1(R[c]["ns5"].reshape(128, 2, 32, 16)[:, 1]).transpose(2, 0, 1) for c in range(8)])[None].astype(f)
    ns_ssd = np.concatenate([R[c]["ns_ssd"].reshape(16, 16, 64, 128) for c in range(8)])[None].astype(f)
    ns_conv = np.concatenate([np.concatenate([R[c]["ns_conv_a"].reshape(16, 2, 1536),
                                              R[c]["ns_conv_bT"].reshape(128, 12, 16).transpose(2, 1, 0).reshape(16, 1, 1536)], axis=1)
                              for c in range(8)])[None].astype(f)
    return (np.ascontiguousarray(y_p), np.ascontiguousarray(y_s), np.ascontiguousarray(np_re), np.ascontiguousarray(np_im),
            np.ascontiguousarray(np_ssd), np.ascontiguousarray(np_conv), np.ascontiguousarray(ns_re), np.ascontiguousarray(ns_im),
            np.ascontiguousarray(ns_ssd), np.ascontiguousarray(ns_conv))
```

```python
import math
from contextlib import ExitStack

import numpy as np
import concourse.bass as bass
import concourse.mybir as mybir
from concourse.bass_utils import run_bass_kernel_spmd

F32 = mybir.dt.float32
BF16 = mybir.dt.bfloat16
I32 = mybir.dt.int32
AF = mybir.ActivationFunctionType
ALU = mybir.AluOpType
AX = mybir.AxisListType

T = 2048
NS = 16
FAST = ('dense', 'glu', 'inproj', 'p1', 'ssd', 'smp', 'x', 'y', 'kmat')
EPS = 1e-5
TWO_PI = 2.0 * math.pi
GELU_C = 2.0 * math.sqrt(2.0 / math.pi)

PC_GMIX, PC_S5N, PC_BGLU, PC_D5, PC_SSDN, PC_CW, PC_CB, PC_FFN, PC_JV, PC_N = 0, 8, 16, 24, 32, 40, 88, 100, 108, 117
PB_FIN, PB_DTB, PB_ALOG, PB_DSSD, PB_N = 0, 1024, 1040, 1056, 1072
L1_LRE, L1_LIM, L1_LDT, L1_CRE, L1_CIM, L1_BRE, L1_BIM, L1_N = 0, 32, 64, 96, 1120, 2144, 3168, 4192


SPLIT = {}


def _keys(x):
    if isinstance(x, str):
        return [x]
    name = x.name
    sp = SPLIT.get(name)
    if sp is None:
        return [name]
    per, sub = sp
    try:
        ap = x.ap
        start = int(x.offset) % per
        ext = 1
        for (st_, cnt) in ap[1:]:
            ext += (cnt - 1) * abs(st_)
    except Exception:
        return ["%s#%d" % (name, i) for i in range(per // sub)]
    lo, hi = start // sub, min(per - 1, start + ext - 1) // sub
    return ["%s#%d" % (name, i) for i in range(lo, hi + 1)]


class Emit:
    ENG = ['pe', 'dve', 'act', 'pool', 'sp']

    def __init__(self, nc):
        self.nc = nc
        self.prog = {e: [] for e in self.ENG}
        self.sem = {e: nc.alloc_semaphore('c_' + e) for e in self.ENG}
        self.cnt = {e: 0 for e in self.ENG}
        self.waited = {e: {} for e in self.ENG}
        self.lastw = {}
        self.readers = {}
        self.dsem = {}
        self.dcnt = {}
        self.deng = {}
        self.nwait = 0
        self.pe_fast = False
        self.skip_groups = set()

    def _deps(self, reads, writes):
        deps = []
        for k in reads:
            if k in self.lastw:
                deps.append(self.lastw[k])
        for k in writes:
            if k in self.lastw:
                deps.append(self.lastw[k])
            deps.extend(self.readers.get(k, []))
        return deps

    def _emit_waits(self, eng, deps):
        w = self.waited[eng]
        need = {}
        for (s, v) in deps:
            if w.get(s, 0) < v and need.get(s, 0) < v:
                need[s] = v
        for s, v in need.items():
            w[s] = v
            self.nwait += 1
            self.prog[eng].append(('wait', s, v))

    def _record(self, dep, reads, writes):
        for k in writes:
            self.lastw[k] = dep
            self.readers[k] = []
        for k in reads:
            if k not in writes:
                self.readers.setdefault(k, []).append(dep)

    def op(self, eng, fn, reads=(), writes=()):
        reads = [k for r in reads for k in _keys(r)]
        writes = [k for r in writes for k in _keys(r)]
        deps = self._deps(reads, writes)
        if eng == 'pe' and self.pe_fast:
            deps = [d for d in deps if d[0] is not self.sem['pe']]
        self._emit_waits(eng, deps)
        self.cnt[eng] += 1
        dep = (self.sem[eng], self.cnt[eng])
        self.prog[eng].append(('op', fn, self.sem[eng], 1))
        self._record(dep, reads, writes)

    def dma(self, eng, out, in_, reads=(), writes=(), **kw):
        reads = [k for r in reads for k in _keys(r)]
        g = _keys(writes[0])[0].split('#')[0]
        writes = [k for r in writes for k in _keys(r)]
        self._emit_waits(eng, self._deps(reads, writes))
        if g not in self.dsem:
            self.dsem[g] = self.nc.alloc_semaphore('d_%d' % len(self.dsem))
            self.dcnt[g] = 0
        self.dcnt[g] += 16
        self.deng[g] = eng
        dep = (self.dsem[g], self.dcnt[g])
        self.prog[eng].append(('op', lambda e: e.dma_start(out=out, in_=in_, **kw), self.dsem[g], 16))
        self._record(dep, reads, writes)

    def barrier(self, exclude=None):
        engs = [e for e in self.ENG if e != exclude]
        deps = [(self.sem[e], self.cnt[e]) for e in engs if self.cnt[e] > 0]
        for g, s in self.dsem.items():
            if self.deng.get(g) != exclude and g not in self.skip_groups:
                deps.append((s, self.dcnt[g]))
        for e in engs:
            if e != 'pe':
                self._emit_waits(e, deps)

    def finish(self, eng='sp'):
        deps = list(self.lastw.values())
        for r in self.readers.values():
            deps.extend(r)
        self._emit_waits(eng, deps)

    def build(self):
        nc = self.nc
        prog = self.prog

        def run(e, lst):
            for it in lst:
                if it[0] == 'wait':
                    e.wait_ge(it[1], it[2])
                else:
                    it[1](e).then_inc(it[2], it[3])

        with nc.Block() as block:
            @block.sync
            def _(e):
                run(e, prog['sp'])

            @block.tensor
            def _(e):
                run(e, prog['pe'])

            @block.vector
            def _(e):
                run(e, prog['dve'])

            @block.scalar
            def _(e):
                run(e, prog['act'])

            @block.gpsimd
            def _(e):
                run(e, prog['pool'])


def build_program(debug=()):
    nc = bass.Bass("TRN2", target_bir_lowering=False)
    E = Emit(nc)
    st = ExitStack()

    def din(name, shape):
        return nc.dram_tensor(name, list(shape), F32, kind="ExternalInput")

    def dout(name, shape):
        return nc.dram_tensor(name, list(shape), F32, kind="ExternalOutput")

    xp_tm = din("xp_tm", [T, 1024]); xp_T = din("xp_T", [1024, T])
    xs_tm = din("xs_tm", [NS, 1024]); xs_T = din("xs_T", [1024, NS])
    pcol_d = din("pcol", [128, PC_N]); pbc_d = din("pbc", [128, PB_N]); p16_d = din("p16", [16, 1027])
    cst_d = din("cst", [128, 256])
    l1_d = din("l1pack", [128, L1_N]); g_d = din("gpack", [64, 129]); b2_d = din("b2pack", [128, 2048])
    h0l1_d = din("h0l1", [128, 1024])
    ssd0_d = din("ssd0", [NS, 1024, 128]); conv0T_d = din("conv0T", [128, 12 * 3 * NS]); conv0_d = din("conv0", [NS, 3 * 1536])
    w_in_d = din("w_in", [1024, 3600]); w_glu_d = din("w_glu", [1024, 1024]); w_out_d = din("w_out", [2048, 1024])
    w_ff1_d = din("w_ff1", [1024, 4096]); w_ff2_d = din("w_ff2", [4096, 1024])
    scr_d = nc.dram_tensor("scr_pf", [64, 1024], F32, kind="Internal")
    wb_in = nc.dram_tensor("wb_in", [1024, 3600], BF16, kind="Internal")
    wb_glu = nc.dram_tensor("wb_glu", [1024, 1024], BF16, kind="Internal")
    wb_out = nc.dram_tensor("wb_out", [2048, 1024], BF16, kind="Internal")
    wb_ff1 = nc.dram_tensor("wb_ff1", [1024, 4096], BF16, kind="Internal")
    wb_ff2 = nc.dram_tensor("wb_ff2", [4096, 1024], BF16, kind="Internal")

    y_p_d = dout("y_p", [T, 1024]); y_s_d = dout("y_s", [NS, 1024])
    np5_d = dout("np5", [128, 64]); np_ssdT_d = dout("np_ssdT", [128, 1024]); np_convT_d = dout("np_convT", [128, 36])
    ns5_d = dout("ns5", [128, 1024]); ns_ssd_d = dout("ns_ssd", [NS, 1024, 128])
    ns_conv_a_d = dout("ns_conv_a", [NS, 2 * 1536]); ns_conv_bT_d = dout("ns_conv_bT", [128, 12 * NS])
    dbg_d = {}

    uniq = [0]

    def sbx(stack, name, shape, dt=F32):
        uniq[0] += 1
        return stack.enter_context(nc.sbuf_tensor("s%d_%s" % (uniq[0], name), list(shape), dt))

    def sb(name, shape, dt=F32):
        return sbx(st, name, shape, dt)

    banks = [st.enter_context(nc.psum_tensor("pb%d" % i, [128, 512], F32)) for i in range(8)]
    bank_i = [0]

    def bank(avoid=None):
        b = banks[bank_i[0] % 8]
        bank_i[0] += 1
        if avoid is not None and b.name == avoid.name:
            b = banks[bank_i[0] % 8]
            bank_i[0] += 1
        return b

    def aps(*xs):
        return [x for x in xs if x is not None and not isinstance(x, (int, float))]

    def TT(eng, out, in0, in1, op):
        E.op(eng, lambda e: e.tensor_tensor(out=out, in0=in0, in1=in1, op=op), reads=aps(in0, in1), writes=[out])

    def TS(eng, out, in0, s1, s2=None, op0=ALU.mult, op1=None):
        kw = {}
        if op1 is not None:
            kw['op1'] = op1
        E.op(eng, lambda e: e.tensor_scalar(out=out, in0=in0, scalar1=s1, scalar2=s2, op0=op0, **kw),
             reads=aps(in0, s1, s2), writes=[out])

    def STT(out, in0, scalar, in1, op0, op1, eng='dve'):
        E.op(eng, lambda e: e.scalar_tensor_tensor(out=out, in0=in0, scalar=scalar, in1=in1, op0=op0, op1=op1),
             reads=aps(in0, scalar, in1), writes=[out])

    def ACT(out, in_, func, bias=None, scale=1.0, accum_out=None):
        kw = {}
        if bias is not None:
            kw['bias'] = bias
        if accum_out is not None:
            kw['accum_out'] = accum_out
        E.op('act', lambda e: e.activation(out=out, in_=in_, func=func, scale=scale, **kw),
             reads=aps(in_, bias, scale), writes=aps(out, accum_out))

    def CP(eng, out, in_):
        if eng == 'act':
            E.op('act', lambda e: e.copy(out=out, in_=in_), reads=[in_], writes=[out])
        else:
            E.op(eng, lambda e: e.tensor_copy(out=out, in_=in_), reads=[in_], writes=[out])

    def MSET(eng, out, val):
        E.op(eng, lambda e: e.memset(out, val), writes=[out])

    def MM(out, lhsT, rhs, start, stop, tp=None):
        kw = {}
        if tp is not None:
            kw['tile_position'] = tp
        E.op('pe', lambda e: e.matmul(out, lhsT=lhsT, rhs=rhs, start=start, stop=stop, **kw),
             reads=[lhsT, rhs], writes=[out])

    def TR(out, in_, ident):
        E.op('pe', lambda e: e.transpose(out, in_, ident), reads=[in_, ident], writes=[out])

    dq = ['sp']

    def LD(out, in_, eng=None, **kw):
        E.dma(eng or dq[0], out, in_, writes=[out], **kw)

    def STo(out, in_, eng=None):
        E.dma(eng or dq[0], out, in_, reads=[in_], writes=[out])

    def RED(out, in_, eng='dve'):
        E.op(eng, lambda e: e.tensor_reduce(out=out, in_=in_, axis=AX.X, op=ALU.add), reads=[in_], writes=[out])

    def RECIP(out, in_):
        E.op('dve', lambda e: e.reciprocal(out=out, in_=in_), reads=[in_], writes=[out])

    def rstd_from(out, in_, n, tmp):
        TS('dve', tmp, in_, 1.0 / n, EPS, ALU.mult, ALU.add)
        ACT(tmp, tmp, AF.Ln)
        ACT(out, tmp, AF.Exp, scale=-0.5)

    def bcl(ap, shape):
        return ap.rearrange("p (f o) -> p f o", o=1).to_broadcast(list(shape))

    def bcm(ap, shape):
        return ap.rearrange("p (o f) -> p o f", o=1).to_broadcast(list(shape))

    def wunit(ring, ri, dram, k0, c0, ncols=512, bf=None):
        t = ring[ri[0] % len(ring)]
        ri[0] += 1
        if bf is None:
            src = dram.ap()[k0:k0 + 1024, c0:c0 + ncols].rearrange("(kt p) c -> p kt c", p=128)
            E.dma('pool', t[:, :, 0:ncols], src, writes=[t])
        else:
            src = bf.ap()[k0:k0 + 1024, c0:c0 + ncols].rearrange("(kt p) c -> p kt c", p=128)
            E.dma('sp', t[:, :, 0:ncols], src, reads=[bf.ap()], writes=[t])
        return t

    def load_hnT(xT_dram, c0, n, hnT, rbc=None):
        E.dma('pool', hnT[:, :, 0:n], xT_dram.ap()[:, c0:c0 + n].rearrange("(kt p) t -> p kt t", p=128), writes=[hnT])
        if rbc is not None:
            for kt in range(8):
                STT(hnT[:, kt, 0:n], hnT[:, kt, 0:n], pcol[:, PC_GMIX + kt:PC_GMIX + kt + 1], rbc, ALU.mult, ALU.mult)
        else:
            TT('dve', hnT[:, :, 0:n], hnT[:, :, 0:n], bcl(pcol[:, PC_GMIX:PC_GMIX + 8], [128, 8, n]), ALU.mult)

    def dbg(name, ap, shape, stack=None, dt=BF16):
        if name in debug:
            d = nc.dram_tensor("dbg_" + name, list(shape), dt, kind="ExternalOutput")
            dbg_d[name] = d
            STo(d.ap(), ap)

    def convert_weights():
        for src, dst, rows in ((w_glu_d, wb_glu, 1024), (w_in_d, wb_in, 1024), (w_out_d, wb_out, 2048),
                               (w_ff1_d, wb_ff1, 1024), (w_ff2_d, wb_ff2, 4096)):
            for r0 in range(0, rows, 512):
                E.dma('pool', dst.ap()[r0:r0 + 512, :], src.ap()[r0:r0 + 512, :], writes=[dst.ap()])
            E.skip_groups.add(dst.ap().name)

    cst = sb("cst", [128, 256]); ident_f = cst[:, 0:128]; tri_f = cst[:, 128:256]
    ident_b = sb("ident_b", [128, 128], BF16)
    ones_f = sb("ones_f", [128, 128])
    pcol = sb("pcol", [128, PC_N])
    uT = [sb("uT%d" % o, [128, T + NS], BF16) for o in range(8)]
    rstdbc_all = sb("rstdbc_all", [128, T]); rstdbc_s = sb("rstdbc_s", [128, NS])
    lam18 = sb("lam18", [128, 4, 32])
    HT = sb("HT", [128, 1024]); HTb = sb("HTb", [128, 1024], BF16)

    LD(cst[:], cst_d.ap()); LD(pcol[:], pcol_d.ap())
    CP('dve', ident_b[:], ident_f)
    MSET('pool', ones_f[:], 1.0)

    with ExitStack() as st1:
        E.pe_fast = 'p1' in FAST
        hnT = [sbx(st1, "p1hn%d" % i, [128, 8, 512], BF16) for i in range(2)]
        sqb = [sbx(st1, "p1sq%d" % i, [128, 8, 512], BF16) for i in range(2)]
        ones_b = sbx(st1, "p1ones", [128, 128], BF16)
        CP('dve', ones_b[:], ones_f[:])
        ring1 = [sbx(st1, "p1w%d" % i, [128, 8, 512], BF16) for i in range(2)]
        r1i = [0]
        wu = [wunit(ring1, r1i, w_in_d, 0, 0), wunit(ring1, r1i, w_in_d, 0, 512)]
        rts = [sbx(st1, "p1rt%d" % i, [128, 512]) for i in range(2)]

        def p1_prep(blk):
            smp = blk == 4
            n = NS if smp else 512
            hn = hnT[blk % 2]; sq = sqb[blk % 2]
            E.dma('pool', hn[:, :, 0:n], (xs_T if smp else xp_T).ap()[:, (0 if smp else blk * 512):(0 if smp else blk * 512) + n]
                  .rearrange("(kt p) t -> p kt t", p=128), writes=[hn])
            ACT(sq[:, :, 0:n], hn[:, :, 0:n], AF.Square)
            ps = bank()
            for kt in range(8):
                MM(ps[:, 0:n], lhsT=ones_b[:], rhs=sq[:, kt, 0:n], start=(kt == 0), stop=(kt == 7))
            rbc = rstdbc_s[:, :] if smp else rstdbc_all[:, blk * 512:(blk + 1) * 512]
            rstd_from(rbc, ps[:, 0:n], 1024.0, rts[blk % 2][:, 0:n])
            for kt in range(8):
                STT(hn[:, kt, 0:n], hn[:, kt, 0:n], pcol[:, PC_GMIX + kt:PC_GMIX + kt + 1], rbc, ALU.mult, ALU.mult)

        def p1_mm(blk):
            smp = blk == 4
            n = NS if smp else 512
            hn = hnT[blk % 2]
            c0 = T if smp else blk * 512
            for o in range(8):
                ps = bank()
                for kt in range(8):
                    MM(ps[:, 0:n], lhsT=wu[o // 4][:, kt, (o % 4) * 128:(o % 4) * 128 + 128], rhs=hn[:, kt, 0:n],
                       start=(kt == 0), stop=(kt == 7))
                if smp:
                    CP('dve', uT[o][:, c0:c0 + n], ps[:, 0:n])
                else:
                    CP('act' if o % 2 == 0 else 'dve', uT[o][:, 0:T].rearrange("p (s k) -> p k s", s=8)[:, 64 * blk:64 * blk + 64, :],
                       ps[:, 0:512].rearrange("p (k s) -> p k s", s=8))

        p1_prep(0)
        for blk in range(5):
            if blk < 4:
                p1_prep(blk + 1)
            p1_mm(blk)
    E.pe_fast = False
    E.barrier()
    if 'u' in debug:
        d = nc.dram_tensor("dbg_u", [128, 8, T + NS], BF16, kind="ExternalOutput"); dbg_d['u'] = d
        for o in range(8):
            STo(d.ap()[:, o, :], uT[o][:])

    def powers(stk, P, Fd, lre, lim, ldt, ar, ai, tg):
        dtt = sbx(stk, tg + "dt", [P, Fd]); lrdt = sbx(stk, tg + "lrdt", [P, Fd]); lidt = sbx(stk, tg + "lidt", [P, Fd])
        mag = sbx(stk, tg + "mag", [P, 9, Fd]); r = sbx(stk, tg + "r", [P, 9, Fd]); rn = sbx(stk, tg + "rn", [P, 9, Fd])
        ri = sbx(stk, tg + "ri", [P, 9, Fd], I32); ng = sbx(stk, tg + "ng", [P, 9, Fd])
        jb = bcl(pcol[0:P, PC_JV:PC_JV + 9], [P, 9, Fd])
        ACT(dtt[:], ldt, AF.Exp)
        TT('dve', lrdt[:], lre, dtt[:], ALU.mult)
        TT('dve', lidt[:], lim, dtt[:], ALU.mult)
        TT('dve', mag[:], bcm(lrdt[:], [P, 9, Fd]), jb, ALU.mult)
        ACT(mag[:], mag[:], AF.Exp)
        for dst, off in ((ai, 0.0), (ar, 0.25)):
            TT('dve', r[:], bcm(lidt[:], [P, 9, Fd]), jb, ALU.mult)
            TS('dve', r[:], r[:], 1.0 / TWO_PI, off, ALU.mult, ALU.add)
            CP('dve', ri[:], r[:])
            CP('dve', rn[:], ri[:])
            TT('dve', r[:], r[:], rn[:], ALU.subtract)
            TS('dve', ng[:], r[:], 0.0, None, ALU.is_lt)
            TT('dve', r[:], r[:], ng[:], ALU.add)
            TS('dve', ng[:], r[:], 1.0, None, ALU.is_ge)
            TT('dve', r[:], r[:], ng[:], ALU.subtract)
            sc = (1.0 - 1e-6)
            TS('dve', r[:], r[:], -TWO_PI * sc, math.pi * sc, ALU.mult, ALU.add)
            ACT(rn[:], r[:], AF.Sin)
            TT('dve', dst, rn[:], mag[:], ALU.mult)

    def fcoef(stk, P, Fd, lre, lim, ar1, ai1, fr, fi, tg):
        den = sbx(stk, tg + "den", [P, Fd]); t1 = sbx(stk, tg + "t1", [P, Fd]); nr = sbx(stk, tg + "nr", [P, Fd])
        TT('dve', den[:], lre, lre, ALU.mult)
        TT('dve', t1[:], lim, lim, ALU.mult)
        TT('dve', den[:], den[:], t1[:], ALU.add)
        RECIP(den[:], den[:])
        TS('dve', nr[:], ar1, -1.0, None, ALU.add)
        TT('dve', fr, nr[:], lre, ALU.mult)
        TT('dve', t1[:], ai1, lim, ALU.mult)
        TT('dve', fr, fr, t1[:], ALU.add)
        TT('dve', fr, fr, den[:], ALU.mult)
        TT('dve', fi, ai1, lre, ALU.mult)
        TT('dve', t1[:], nr[:], lim, ALU.mult)
        TT('dve', fi, fi, t1[:], ALU.subtract)
        TT('dve', fi, fi, den[:], ALU.mult)

    stS5 = ExitStack()
    XH = [sbx(stS5, "XH%d" % h, [128, 2, 16, 257], BF16) for h in range(2)]
    Xs_sb = sbx(stS5, "Xs_sb", [128, 2, 32, NS])
    with ExitStack() as stW:
        WXt = sbx(stW, "WXt", [128, 8, 8, 2, 128], BF16)
        with ExitStack() as st0:
            gp = sbx(st0, "gp", [64, 129]); b2 = sbx(st0, "b2", [128, 2048])
            LD(gp[:], g_d.ap()); LD(b2[:], b2_d.ap())
            arG = sbx(st0, "arG", [64, 9, 64]); aiG = sbx(st0, "aiG", [64, 9, 64])
            frG = sbx(st0, "frG", [64, 64]); fiG = sbx(st0, "fiG", [64, 64])
            powers(st0, 64, 64, gp[:, 0:64], gp[:, 64:128], gp[:, 128:129].to_broadcast([64, 64]), arG[:], aiG[:], "G")
            fcoef(st0, 64, 64, gp[:, 0:64], gp[:, 64:128], arG[:, 1, :], aiG[:, 1, :], frG[:], fiG[:], "G")
            PfG = sbx(st0, "PfG", [64, 8, 2, 64]); tG = sbx(st0, "tG", [64, 8, 64])
            frb = bcm(frG[:], [64, 8, 64]); fib = bcm(fiG[:], [64, 8, 64])
            TT('dve', PfG[:, :, 0, :], arG[:, 0:8, :], frb, ALU.mult)
            TT('dve', tG[:], aiG[:, 0:8, :], fib, ALU.mult)
            TT('dve', PfG[:, :, 0, :], PfG[:, :, 0, :], tG[:], ALU.subtract)
            TT('dve', PfG[:, :, 1, :], arG[:, 0:8, :], fib, ALU.mult)
            TT('dve', tG[:], aiG[:, 0:8, :], frb, ALU.mult)
            TT('dve', PfG[:, :, 1, :], PfG[:, :, 1, :], tG[:], ALU.add)
            STo(scr_d.ap(), PfG[:].rearrange("p a b c -> p (a b c)"))
            PfL2 = sbx(st0, "PfL2", [128, 8, 16, 64])
            for g8 in range(8):
                src = bass.AP(scr_d, g8 * 1024, [[0, 16], [8 * 1024, 8], [64, 16], [1, 64]])
                E.dma('sp', PfL2[16 * g8:16 * g8 + 16, :, :, :], src, reads=[scr_d.ap()], writes=[PfL2])
            tA = sbx(st0, "tA", [128, 1024]); tB = sbx(st0, "tB", [128, 1024])
            tA2 = sbx(st0, "tA2", [128, 1024]); tB2 = sbx(st0, "tB2", [128, 1024])
            b2re = b2[:, 0:1024].rearrange("p (o g q) -> p o g q", o=8, g=2)
            b2im = b2[:, 1024:2048].rearrange("p (o g q) -> p o g q", o=8, g=2)
            for s in range(8):
                j = 7 - s
                pre = PfL2[:, :, 2 * j, :].rearrange("p o (g q) -> p o g q", g=1).to_broadcast([128, 8, 2, 64])
                pim = PfL2[:, :, 2 * j + 1, :].rearrange("p o (g q) -> p o g q", g=1).to_broadcast([128, 8, 2, 64])
                e1, ta, tb = ('dve', tA, tB) if s % 3 != 2 else ('pool', tA2, tB2)
                tAv = ta[:].rearrange("p (o g q) -> p o g q", o=8, g=2)
                tBv = tb[:].rearrange("p (o g q) -> p o g q", o=8, g=2)
                TT(e1, tAv, b2re, pre, ALU.mult)
                TT(e1, tBv, b2im, pim, ALU.mult)
                TT(e1, WXt[:, :, s, 0, :].rearrange("p o (g q) -> p o g q", g=2), tAv, tBv, ALU.subtract)
                TT(e1, tAv, b2im, pre, ALU.mult)
                TT(e1, tBv, b2re, pim, ALU.mult)
                TT(e1, WXt[:, :, s, 1, :].rearrange("p o (g q) -> p o g q", g=2), tAv, tBv, ALU.add)
        dbg('WXt', WXt[:].rearrange("p a b c d -> p (a b c d)"), [128, 16384])
        E.barrier()
        E.pe_fast = 'x' in FAST
        for o in range(8):
            psq = [bank() for _ in range(4)]
            for ri in range(2):
                for s in range(8):
                    for jj in range(4):
                        MM(psq[jj][:, ri * 256:(ri + 1) * 256], lhsT=WXt[32 * jj:32 * jj + 32, o, s, ri, :],
                           rhs=uT[o][32 * jj:32 * jj + 32, s * 256:(s + 1) * 256], start=(s == 0), stop=(s == 7), tp=(32 * jj, 0))
            for jj in range(4):
                pair = 4 * o + jj
                CP('dve', XH[pair // 16][:, :, pair % 16, 1:257], psq[jj][:, :].rearrange("p (r k) -> p r k", r=2))
        psx = [bank() for _ in range(4)]
        for pair in range(32):
            o, jj = pair // 4, pair % 4
            for ri in range(2):
                MM(psx[jj][:, (ri * 8 + o) * NS:(ri * 8 + o + 1) * NS], lhsT=WXt[32 * jj:32 * jj + 32, o, 7, ri, :],
                   rhs=uT[o][32 * jj:32 * jj + 32, T:T + NS], start=True, stop=True, tp=(32 * jj, 0))
        for jj in range(4):
            for ri in range(2):
                CP('dve', Xs_sb[:, ri, jj:32:4, :], psx[jj][:, ri * 128:(ri + 1) * 128].rearrange("p (a b) -> p a b", b=NS))
    dbg('X0', XH[0][:].rearrange("p a b c -> p (a b c)"), [128, 2 * 16 * 257])
    E.pe_fast = False
    E.barrier()

    WCz = sbx(stS5, "WCz", [128, 32, 9, 2, 32], BF16)
    BBz = sbx(stS5, "BBz", [128, 32, 2, 32], BF16)
    Kmat = [sbx(stS5, "Kmat%d" % o, [128, 8, 128], BF16) for o in range(8)]
    with ExitStack() as st0:
        l1 = sbx(st0, "l1", [128, L1_N])
        LD(l1[:], l1_d.ap())
        arL = sbx(st0, "arL", [128, 9, 32]); aiL = sbx(st0, "aiL", [128, 9, 32])
        frL = sbx(st0, "frL", [128, 32]); fiL = sbx(st0, "fiL", [128, 32])
        powers(st0, 128, 32, l1[:, L1_LRE:L1_LRE + 32], l1[:, L1_LIM:L1_LIM + 32], l1[:, L1_LDT:L1_LDT + 32], arL[:], aiL[:], "L")
        fcoef(st0, 128, 32, l1[:, L1_LRE:L1_LRE + 32], l1[:, L1_LIM:L1_LIM + 32], arL[:, 1, :], aiL[:, 1, :], frL[:], fiL[:], "L")
        CP('dve', lam18[:, 0, :], arL[:, 1, :]); CP('dve', lam18[:, 1, :], aiL[:, 1, :])
        CP('dve', lam18[:, 2, :], arL[:, 8, :]); CP('dve', lam18[:, 3, :], aiL[:, 8, :])
        czre = l1[:, L1_CRE:L1_CRE + 1024].rearrange("p (a c) -> p a c", c=32)
        czim = l1[:, L1_CIM:L1_CIM + 1024].rearrange("p (a c) -> p a c", c=32)
        bzre = l1[:, L1_BRE:L1_BRE + 1024].rearrange("p (a c) -> p a c", c=32)
        bzim = l1[:, L1_BIM:L1_BIM + 1024].rearrange("p (a c) -> p a c", c=32)
        tCs = [sbx(st0, "tC%d" % i, [128, 32, 32]) for i in range(2)]
        tDs = [sbx(st0, "tD%d" % i, [128, 32, 32]) for i in range(2)]
        for j in range(9):
            arb = bcl(arL[:, j, :], [128, 32, 32]); aib = bcl(aiL[:, j, :], [128, 32, 32])
            e1 = 'dve' if j % 3 != 2 else 'pool'
            tC, tD = (tCs[0], tDs[0]) if e1 == 'dve' else (tCs[1], tDs[1])
            TT(e1, tC[:], czre, arb, ALU.mult)
            TT(e1, tD[:], czim, aib, ALU.mult)
            TT(e1, WCz[:, :, j, 0, :], tC[:], tD[:], ALU.subtract)
            TT(e1, tC[:], czre, aib, ALU.mult)
            TT(e1, tD[:], czim, arb, ALU.mult)
            TT(e1, tC[:], tC[:], tD[:], ALU.add)
            TS(e1, WCz[:, :, j, 1, :], tC[:], -1.0)
        tC, tD = tCs[0], tDs[0]
        frb = bcl(frL[:], [128, 32, 32]); fib = bcl(fiL[:], [128, 32, 32])
        TT('dve', tC[:], bzre, frb, ALU.mult)
        TT('dve', tD[:], bzim, fib, ALU.mult)
        TT('dve', BBz[:, :, 0, :], tC[:], tD[:], ALU.subtract)
        TT('dve', tC[:], bzre, fib, ALU.mult)
        TT('dve', tD[:], bzim, frb, ALU.mult)
        TT('dve', BBz[:, :, 1, :], tC[:], tD[:], ALU.add)
        E.pe_fast = 'kmat' in FAST
        for o in range(8):
            ps = bank()
            MSET('pool', Kmat[o][:], 0.0)
            for jj in range(4):
                pair = 4 * o + jj
                for ri in range(2):
                    MM(ps[32 * jj:32 * jj + 32, 0:256].rearrange("p (a c) -> p a c", c=32),
                       lhsT=BBz[:, pair, ri, :], rhs=WCz[:, pair, 0:8, ri, :],
                       start=(ri == 0), stop=(ri == 1), tp=(0, 32 * jj))
            for jj in range(4):
                CP('dve', Kmat[o][32 * jj:32 * jj + 32, :, 32 * jj:32 * jj + 32],
                   ps[32 * jj:32 * jj + 32, 0:256].rearrange("p (a c) -> p a c", c=32))
            STT(Kmat[o][:, 0, :], ident_f, pcol[:, PC_D5 + o:PC_D5 + o + 1], Kmat[o][:, 0, :], ALU.mult, ALU.add)
    E.pe_fast = False
    dbg('lam18', lam18[:].rearrange("p a b -> p (a b)"), [128, 128], None, F32)
    dbg('Kmat0', Kmat[0][:].rearrange("p a b -> p (a b)"), [128, 1024])
    dbg('WCz', WCz[:].rearrange("p a b c d -> p (a b c d)"), [128, 18432])
    E.barrier()

    def gelu_evac(out, ps_ap, t1, t2):
        ACT(t1, ps_ap, AF.Square)
        TS('dve', t1, t1, 0.044715, 1.0, ALU.mult, ALU.add)
        TT('dve', t1, t1, ps_ap, ALU.mult)
        ACT(t2, t1, AF.Sigmoid, scale=GELU_C)
        TT('dve', out, t2, ps_ap, ALU.mult)

    with ExitStack() as st2:
        h0 = sbx(st2, "h0", [128, 2, 32, NS]); h0b = sbx(st2, "h0b", [128, 2, 32, NS], BF16)
        ns5 = sbx(st2, "ns5", [128, 2, 32, NS]); tS = sbx(st2, "tS", [128, 32, NS])
        LD(h0[:].rearrange("p a b c -> p (a b c)"), h0l1_d.ap())
        CP('dve', h0b[:], h0[:])
        ar1b = bcl(lam18[:, 0, :], [128, 32, NS]); ai1b = bcl(lam18[:, 1, :], [128, 32, NS])
        for ri in range(2):
            TT('dve', ns5[:, ri, :, :], h0[:, ri, :, :], ar1b, ALU.mult)
            TT('dve', ns5[:, ri, :, :], ns5[:, ri, :, :], Xs_sb[:, ri, :, :], ALU.add)
        TT('dve', tS[:], h0[:, 1, :, :], ai1b, ALU.mult)
        TT('dve', ns5[:, 0, :, :], ns5[:, 0, :, :], tS[:], ALU.subtract)
        TT('dve', tS[:], h0[:, 0, :, :], ai1b, ALU.mult)
        TT('dve', ns5[:, 1, :, :], ns5[:, 1, :, :], tS[:], ALU.add)
        STo(ns5_d.ap(), ns5[:].rearrange("p a b c -> p (a b c)"))
        convert_weights()
        np5 = sbx(st2, "np5", [128, 2, 32])
        with ExitStack() as stH:
            NL = 5
            Dl = [sbx(stH, "Dl%d" % l, [128, 2, 32]) for l in range(NL + 1)]
            dtmp = sbx(stH, "dtmp", [128, 32])
            CP('dve', Dl[0][:, 0, :], lam18[:, 2, :]); CP('dve', Dl[0][:, 1, :], lam18[:, 3, :])
            for l in range(1, NL + 1):
                TT('dve', Dl[l][:, 0, :], Dl[l - 1][:, 0, :], Dl[l - 1][:, 0, :], ALU.mult)
                TT('dve', dtmp[:], Dl[l - 1][:, 1, :], Dl[l - 1][:, 1, :], ALU.mult)
                TT('dve', Dl[l][:, 0, :], Dl[l][:, 0, :], dtmp[:], ALU.subtract)
                TT('dve', Dl[l][:, 1, :], Dl[l - 1][:, 0, :], Dl[l - 1][:, 1, :], ALU.mult)
                TS('dve', Dl[l][:, 1, :], Dl[l][:, 1, :], 2.0)
            Yl = [None] + [[sbx(stH, "Y%d_%d" % (l, h), [128, 2, 16, 256 >> l], BF16 if l <= 2 else F32) for h in range(2)]
                           for l in range(1, NL + 1)]
            ct1 = sbx(stH, "ct1", [128, 16, 64]); ct2 = sbx(stH, "ct2", [128, 16, 64])

            def lv(l, h, ri, sl):
                if l == 0:
                    return XH[h][:, ri, :, slice(sl.start + 1, sl.stop + 1, sl.step)]
                return Yl[l][h][:, ri, :, sl]

            def cma(h, l, o_re, o_im, a_re, a_im, b_re, b_im, M):
                for m0 in range(0, M, 64):
                    mm = min(64, M - m0)
                    sl = slice(m0, m0 + mm)
                    dr = bcl(Dl[l][:, 0, 16 * h:16 * h + 16], [128, 16, mm])
                    di = bcl(Dl[l][:, 1, 16 * h:16 * h + 16], [128, 16, mm])
                    t1 = ct1[:, :, 0:mm]; t2 = ct2[:, :, 0:mm]
                    TT('dve', t1, a_re[:, :, sl], dr, ALU.mult)
                    TT('dve', t2, a_im[:, :, sl], di, ALU.mult)
                    TT('dve', t1, t1, t2, ALU.subtract)
                    TT('dve', t2, a_im[:, :, sl], dr, ALU.mult)
                    TT('dve', o_re[:, :, sl], t1, b_re[:, :, sl], ALU.add)
                    TT('dve', t1, a_re[:, :, sl], di, ALU.mult)
                    TT('dve', t1, t1, t2, ALU.add)
                    TT('dve', o_im[:, :, sl], t1, b_im[:, :, sl], ALU.add)

            for h in range(2):
                MSET('dve', XH[h][:, :, :, 0:1], 0.0)
            for l in range(NL):
                M = 256 >> l
                for h in range(2):
                    ev = slice(0, M, 2); od = slice(1, M, 2)
                    cma(h, l, Yl[l + 1][h][:, 0], Yl[l + 1][h][:, 1], lv(l, h, 0, ev), lv(l, h, 1, ev),
                        lv(l, h, 0, od), lv(l, h, 1, od), M // 2)
            MT = 256 >> NL
            YT = Yl[NL]
            sT = [sbx(stH, "sT%d" % h, [128, 2, 16]) for h in range(2)]
            sU = [sbx(stH, "sU%d" % h, [128, 2, 16]) for h in range(2)]
            A1 = [sbx(stH, "A1_%d" % h, [128, 2, 16]) for h in range(2)]
            A2 = [sbx(stH, "A2_%d" % h, [128, 2, 16]) for h in range(2)]
            for h in range(2):
                for ri in range(2):
                    CP('dve', A1[h][:, ri, :], Dl[NL][:, 0, 16 * h:16 * h + 16])
                TS('dve', A2[h][:, 0, :], Dl[NL][:, 1, 16 * h:16 * h + 16], -1.0)
                CP('dve', A2[h][:, 1, :], Dl[NL][:, 1, 16 * h:16 * h + 16])
            for m in range(1, MT):
                for h in range(2):
                    TT('dve', sT[h][:], YT[h][:, :, :, m - 1], A1[h][:], ALU.mult)
                    TT('dve', sU[h][:, 0, :], YT[h][:, 1, :, m - 1], A2[h][:, 0, :], ALU.mult)
                    TT('dve', sU[h][:, 1, :], YT[h][:, 0, :, m - 1], A2[h][:, 1, :], ALU.mult)
                    TT('dve', sT[h][:], sT[h][:], sU[h][:], ALU.add)
                    TT('dve', YT[h][:, :, :, m], YT[h][:, :, :, m], sT[h][:], ALU.add)
            for h in range(2):
                CP('dve', np5[:, :, 16 * h:16 * h + 16], YT[h][:, :, :, MT - 1])
            for l in range(NL - 1, -1, -1):
                M = 256 >> l
                for h in range(2):
                    ev = slice(2, M, 2)
                    pv = slice(0, M // 2 - 1)
                    cma(h, l, lv(l, h, 0, ev), lv(l, h, 1, ev), Yl[l + 1][h][:, 0, :, pv], Yl[l + 1][h][:, 1, :, pv],
                        lv(l, h, 0, ev), lv(l, h, 1, ev), M // 2 - 1)
                    for ri in range(2):
                        CP('dve', lv(l, h, ri, slice(1, M, 2)), Yl[l + 1][h][:, ri, :, :])
        E.barrier()
        dbg('H0', XH[0][:].rearrange("p a b c -> p (a b c)"), [128, 2 * 16 * 257])
        STo(np5_d.ap(), np5[:].rearrange("p a b -> p (a b)"))
        E.pe_fast = 'y' in FAST
        g1 = [sbx(st2, "g1_%d" % i, [128, 512]) for i in range(2)]
        g2 = [sbx(st2, "g2_%d" % i, [128, 512]) for i in range(2)]
        gi = 0
        for sp in (6, 4, 2, 0):
            for o in range(8):
                ps = bank()
                for q in range(2):
                    s1 = sp + q
                    reg = ps[:, q * 256:(q + 1) * 256]
                    for s in range(s1 + 1):
                        MM(reg, lhsT=Kmat[o][:, s1 - s, :], rhs=uT[o][:, s * 256:(s + 1) * 256], start=(s == 0), stop=False)
                    for jj in range(4):
                        pair = 4 * o + jj
                        for ri in range(2):
                            MM(ps[32 * jj:32 * jj + 32, q * 256:(q + 1) * 256], lhsT=WCz[:, pair, s1 + 1, ri, :],
                               rhs=XH[pair // 16][:, ri, pair % 16, 0:256], start=False, stop=(ri == 1),
                               tp=(0, 32 * jj))
                outv = uT[o][:, sp * 256:(sp + 2) * 256].rearrange("p (q k) -> p q k", q=2)
                gelu_evac(outv, ps[:, :].rearrange("p (q k) -> p q k", q=2),
                          g1[gi % 2][:].rearrange("p (q k) -> p q k", q=2), g2[gi % 2][:].rearrange("p (q k) -> p q k", q=2))
                gi += 1
        ps = bank()
        for o in range(8):
            reg = ps[:, o * NS:(o + 1) * NS]
            MM(reg, lhsT=Kmat[o][:, 0, :], rhs=uT[o][:, T:T + NS], start=True, stop=False)
            for jj in range(4):
                pair = 4 * o + jj
                for ri in range(2):
                    MM(ps[32 * jj:32 * jj + 32, o * NS:(o + 1) * NS], lhsT=WCz[:, pair, 1, ri, :],
                       rhs=h0b[:, ri, pair, :], start=False, stop=(ri == 1), tp=(0, 32 * jj))
        gs = sbx(st2, "gs", [128, 8 * NS])
        gelu_evac(gs[:], ps[:, 0:8 * NS], g1[0][:, 0:8 * NS], g2[0][:, 0:8 * NS])
        for o in range(8):
            CP('dve', uT[o][:, T:T + NS], gs[:, o * NS:(o + 1) * NS])
    E.pe_fast = False
    stS5.close()
    E.barrier()
    if 'y5' in debug:
        d = nc.dram_tensor("dbg_y5", [128, 8, T + NS], BF16, kind="ExternalOutput"); dbg_d['y5'] = d
        for o in range(8):
            STo(d.ap()[:, o, :], uT[o][:])

    dq[0] = 'pool'
    E.skip_groups.update([y_p_d.ap().name, y_s_d.ap().name, np_ssdT_d.ap().name])
    pbc = sb("pbc", [128, PB_N]); Abc = sb("Abc", [128, 16]); Dident = sb("Dident", [128, 16, 128], BF16)
    diagw = sb("diagw", [128, 4, 12, 128], BF16)
    ring = [sb("wring%d" % i, [128, 8, 512], BF16) for i in range(3)]
    rgi = [0]
    x_tm = sb("x_tm", [128, 4, 1024]); hnT3 = sb("hnT3", [128, 8, 512], BF16); hfT3 = sb("hfT3", [128, 8, 512], BF16)
    SPLIT[x_tm.name] = (4096, 1024)
    ymixT = sb("ymixT", [128, 16, 512], BF16)
    ssq4 = sb("ssq4", [128, 4]); rs4 = sb("rs4", [128, 4]); tmp4b = sb("tmp4b", [128, 4])
    LD(pbc[:], pbc_d.ap())
    MSET('dve', HT[:], 0.0); MSET('dve', HTb[:], 0.0)
    ACT(Abc[:], pbc[:, PB_ALOG:PB_ALOG + 16], AF.Exp)
    TS('dve', Abc[:], Abc[:], -1.0)
    for h in range(16):
        TS('dve', Dident[:, h, :], ident_f, pbc[:, PB_DSSD + h:PB_DSSD + h + 1])
    for k in range(4):
        for ft in range(12):
            c = PC_CW + k * 12 + ft
            TS('dve', diagw[:, k, ft, :], ident_f, pcol[:, c:c + 1])

    carry = sb("carry", [128, 12, 3], BF16)
    NEGm = sb("NEGm", [128, 4, 128])
    for q in range(4):
        TS('dve', NEGm[:, q, :], tri_f, 30000.0, -30000.0, ALU.mult, ALU.add)
    hnT3_s = sb("hnT3_s", [128, 8, NS], BF16); hfT3_s = sb("hfT3_s", [128, 8, NS], BF16)
    ymixT_s = sb("ymixT_s", [128, 16, NS], BF16)
    ssq4s = sb("ssq4s", [16, 1]); rs4s = sb("rs4s", [16, 1]); tmp4s = sb("tmp4s", [16, 1])
    for blk in range(4):
        S3 = blk == 3
        smp = False
        n = 512
        rows = 128
        ntt = 4
        c0 = blk * 512
        ss_small = ExitStack()
        if S3:
            zT = sbx(ss_small, "zT", [128, 8, NS]); xbs = sbx(ss_small, "xbs", [128, 12, NS]); dts = sbx(ss_small, "dts", [16, NS])
        sc_ = ExitStack()
        zs = sbx(sc_, "zs", [128, 4, 1024], BF16)
        xcT = sbx(sc_, "xcT", [128, 12, 512], BF16)
        SPLIT[zs.name] = (4096, 1024); SPLIT[xcT.name] = (12 * 512, 512)
        dtv = sbx(sc_, "dtv", [128, 4, 16]); dta = sbx(sc_, "dta", [128, 4, 16]); dtl = sbx(sc_, "dtl", [128, 4, 16])
        lastx = sbx(sc_, "lastx", [128, 12, 3])
        sx = ExitStack()
        xbcT = sbx(sx, "xbcT", [128, 12, 515], BF16)
        SPLIT[xbcT.name] = (12 * 515, 515)
        E.pe_fast = 'glu' in FAST
        with ExitStack() as sa:
            o5 = [sbx(sa, "o5_%d" % i, [128, 512]) for i in range(8)]
            sq = [sbx(sa, "sq%d" % i, [128, 512]) for i in range(2)]
            gate = [sbx(sa, "gate%d" % i, [128, 512]) for i in range(2)]
            r5 = sbx(sa, "r5", [128, 512]); r5t = sbx(sa, "r5t", [128, 512])

            def ublk(kt, smp=smp, blk=blk):
                if smp:
                    return uT[kt][:, T:T + NS]
                return uT[kt][:, 0:T].rearrange("p (s k) -> p s k", s=8)[:, :, 64 * blk:64 * blk + 64]

            def psv(t, smp=smp):
                if smp:
                    return t[:, 0:NS]
                return t[:, 0:512].rearrange("p (s k) -> p s k", s=8)
            wg = [wunit(ring, rgi, w_glu_d, 0, 0, bf=wb_glu), wunit(ring, rgi, w_glu_d, 0, 512, bf=wb_glu)]
            pss = bank()
            if S3:
                o5s = sbx(sa, "o5s", [128, 8, NS]); sqs_ = [sbx(sa, "sqs%d" % i, [128, NS]) for i in range(2)]
                gts = [sbx(sa, "gts%d" % i, [128, NS]) for i in range(2)]
                r5s = sbx(sa, "r5s", [128, NS]); r5ts = sbx(sa, "r5ts", [128, NS])
                pss_s = bank()
            for oc in range(8):
                ps = bank(avoid=pss)
                if S3 and ps.name == pss_s.name:
                    ps = bank(avoid=pss)
                for kt in range(8):
                    MM(psv(ps), lhsT=wg[oc // 4][:, kt, (oc % 4) * 128:(oc % 4) * 128 + 128], rhs=ublk(kt),
                       start=(kt == 0), stop=(kt == 7))
                ACT(gate[oc % 2][:, 0:n], ps[:, 0:n], AF.Sigmoid, bias=pcol[:, PC_BGLU + oc:PC_BGLU + oc + 1])
                TT('dve', psv(o5[oc]), psv(gate[oc % 2]), ublk(oc), ALU.mult)
                ACT(sq[oc % 2][:, 0:n], o5[oc][:, 0:n], AF.Square)
                MM(pss[:, 0:n], lhsT=ones_f[:, :], rhs=sq[oc % 2][:, 0:n], start=(oc == 0), stop=(oc == 7))
            if S3:
                for oc in range(8):
                    ps2 = bank(avoid=pss)
                    if ps2.name == pss_s.name:
                        ps2 = bank(avoid=pss)
                    for kt in range(8):
                        MM(ps2[:, 0:NS], lhsT=wg[oc // 4][:, kt, (oc % 4) * 128:(oc % 4) * 128 + 128], rhs=uT[kt][:, T:T + NS],
                           start=(kt == 0), stop=(kt == 7))
                    ACT(gts[oc % 2][:], ps2[:, 0:NS], AF.Sigmoid, bias=pcol[:, PC_BGLU + oc:PC_BGLU + oc + 1])
                    TT('dve', o5s[:, oc, :], gts[oc % 2][:], uT[oc][:, T:T + NS], ALU.mult)
                    ACT(sqs_[oc % 2][:], o5s[:, oc, :], AF.Square)
                    MM(pss_s[:, 0:NS], lhsT=ones_f[:, :], rhs=sqs_[oc % 2][:], start=(oc == 0), stop=(oc == 7))
            rstd_from(r5[:, 0:n], pss[:, 0:n], 1024.0, r5t[:, 0:n])
            for oc in range(8):
                if smp:
                    STT(ymixT[:, oc, 0:n], o5[oc][:, 0:n], pcol[:, PC_S5N + oc:PC_S5N + oc + 1], r5[:, 0:n], ALU.mult, ALU.mult)
                else:
                    STT(ymixT[:, oc, :].rearrange("p (k s) -> p s k", s=8), psv(o5[oc]), pcol[:, PC_S5N + oc:PC_S5N + oc + 1],
                        psv(r5), ALU.mult, ALU.mult)
            if S3:
                rstd_from(r5s[:], pss_s[:, 0:NS], 1024.0, r5ts[:])
                for oc in range(8):
                    STT(ymixT_s[:, oc, :], o5s[:, oc, :], pcol[:, PC_S5N + oc:PC_S5N + oc + 1], r5s[:], ALU.mult, ALU.mult)
        if blk == 0:
            dbg('y5n', ymixT[:, 0:8, :], [128, 8, 512])
        E.pe_fast = 'inproj' in FAST
        if smp:
            LD(x_tm[0:NS, 0, :], xs_tm.ap())
        else:
            LD(x_tm[:], xp_tm.ap()[blk * 512:(blk + 1) * 512, :].rearrange("(tt p) f -> p tt f", p=128))
        if blk == 0:
            load_hnT(xp_T, 0, 512, hnT3, rstdbc_all[:, 0:512])
        rbc = rstdbc_s[:, :] if smp else rstdbc_all[:, blk * 512:(blk + 1) * 512]
        if not smp:
            wz = [wunit(ring, rgi, w_in_d, 0, 1024, bf=wb_in), wunit(ring, rgi, w_in_d, 0, 1536, bf=wb_in)]
            for half in range(2):
                for tt in range(4):
                    ps = bank()
                    for kt in range(8):
                        MM(ps[:, :], lhsT=hnT3[:, kt, tt * 128:(tt + 1) * 128], rhs=wz[half][:, kt, :], start=(kt == 0), stop=(kt == 7))
                    ACT(zs[:, tt, half * 512:(half + 1) * 512], ps[:, :], AF.Silu)
                if S3:
                    for oc in range(4 * half, 4 * half + 4):
                        ps = bank()
                        for kt in range(8):
                            MM(ps[:, 0:NS], lhsT=wz[half][:, kt, (oc % 4) * 128:(oc % 4 + 1) * 128], rhs=hnT3_s[:, kt, :], start=(kt == 0), stop=(kt == 7))
                        ACT(zT[:, oc, :], ps[:, 0:NS], AF.Silu)
            if blk == 0:
                MSET('dve', xbcT[:, :, 0:3], 0.0)
            else:
                CP('dve', xbcT[:, :, 0:3], carry[:])
            for un in range(3):
                wx = wunit(ring, rgi, w_in_d, 0, 2048 + 512 * un, bf=wb_in)
                for sub in range(4):
                    ft = 4 * un + sub
                    ps = bank()
                    for kt in range(8):
                        MM(ps[:, :], lhsT=wx[:, kt, sub * 128:(sub + 1) * 128], rhs=hnT3[:, kt, :], start=(kt == 0), stop=(kt == 7))
                    CP('dve', xbcT[:, ft, 3:515], ps[:, :])
                    if blk == 3:
                        CP('dve', lastx[:, ft, :], ps[:, 509:512])
                    if S3:
                        ps = bank()
                        for kt in range(8):
                            MM(ps[:, 0:NS], lhsT=wx[:, kt, sub * 128:(sub + 1) * 128], rhs=hnT3_s[:, kt, :], start=(kt == 0), stop=(kt == 7))
                        CP('dve', xbs[:, ft, :], ps[:, 0:NS])
            CP('dve', carry[:], xbcT[:, :, 512:515])
            if blk == 3:
                STo(np_convT_d.ap(), lastx[:].rearrange("p a b -> p (a b)"))
            wd = wunit(ring, rgi, w_in_d, 0, 3584, 16, bf=wb_in)
            ps = bank()
            for tt in range(4):
                for kt in range(8):
                    MM(ps[:, tt * 16:(tt + 1) * 16], lhsT=hnT3[:, kt, tt * 128:(tt + 1) * 128], rhs=wd[:, kt, 0:16], start=(kt == 0), stop=(kt == 7))
                TT('dve', dtv[:, tt, :], ps[:, tt * 16:(tt + 1) * 16], pbc[:, PB_DTB:PB_DTB + 16], ALU.add)
            if S3:
                ps = bank()
                for kt in range(8):
                    MM(ps[0:16, 0:NS], lhsT=wd[:, kt, 0:16], rhs=hnT3_s[:, kt, :], start=(kt == 0), stop=(kt == 7))
                CP('dve', dts[:], ps[0:16, 0:NS])
            TS('dve', dtl[:], dtv[:], -1.0)
            TT('dve', dtl[:], dtl[:], dtv[:], ALU.min)
            ACT(dtl[:], dtl[:], AF.Exp)
            TS('dve', dtl[:], dtl[:], 1.0, None, ALU.add)
            ACT(dtl[:], dtl[:], AF.Ln)
            STT(dtv[:], dtv[:], 0.0, dtl[:], ALU.max, ALU.add)
            TT('dve', dta[:], dtv[:], bcm(Abc[:], [128, 4, 16]), ALU.mult)
            for ft in range(12):
                ps = bank()
                for k in range(4):
                    MM(ps[:, :], lhsT=diagw[:, k, ft, :], rhs=xbcT[:, ft, k:k + 512], start=(k == 0), stop=(k == 3))
                ACT(xcT[:, ft, :], ps[:, :], AF.Silu, bias=pcol[:, PC_CB + ft:PC_CB + ft + 1])
            if blk == 0:
                dbg('xc', xcT[:], [128, 12, 512], sc_)
            sx.close()
            E.barrier(exclude='sp')
            E.pe_fast = 'ssd' in FAST
            x_t = [sbx(sc_, "x_t%d" % i, [128, 1024], BF16) for i in range(2)]
            B_t = [sbx(sc_, "B_t%d" % i, [128, 256], BF16) for i in range(2)]
            acum = [sbx(sc_, "acum%d" % i, [128, 16]) for i in range(2)]
            nacum = [sbx(sc_, "nacum%d" % i, [128, 16]) for i in range(2)]
            cdb = [sbx(sc_, "cdb%d" % i, [128, 16]) for i in range(2)]
            dsd = [sbx(sc_, "dsd%d" % i, [128, 16]) for i in range(2)]
            ea = [sbx(sc_, "ea%d" % i, [128, 16]) for i in range(2)]
            xdt = [sbx(sc_, "xdt%d" % i, [128, 1024], BF16) for i in range(2)]
            xsd = [sbx(sc_, "xsd%d" % i, [128, 1024], BF16) for i in range(2)]
            CBm = [sbx(sc_, "CBm%d" % i, [128, 2, 128], BF16) for i in range(2)]
            Em = [sbx(sc_, "Em%d" % i, [128, 16, 128], BF16) for i in range(2)]
            diagA = sbx(sc_, "diagA", [128, 16, 128])
            yv = sbx(sc_, "yv", [128, 1024]); yv2 = sbx(sc_, "yv2", [128, 1024]); ynb = sbx(sc_, "ynb", [128, 1024], BF16)

            def ssd_front(c, i):
                cs = slice(c * 128, (c + 1) * 128)
                pst = bank()
                pstb = pst[:, :].bitcast(BF16)
                for ft in range(8):
                    TR(pstb[:, ft * 128:(ft + 1) * 128], xcT[:, ft, cs], ident_b[:])
                CP('dve', x_t[i][:], pstb)
                pst2 = bank()
                pst2b = pst2[:, :].bitcast(BF16)
                for g in range(2):
                    TR(pst2b[:, g * 128:(g + 1) * 128], xcT[:, 8 + g, cs], ident_b[:])
                CP('act', B_t[i][:], pst2b[:, 0:256])
                psa = bank()
                MM(psa[:, 0:16], lhsT=tri_f, rhs=dta[:, c, :], start=True, stop=True)
                MM(psa[:, 16:32], lhsT=ones_f[:, :], rhs=dta[:, c, :], start=True, stop=True)
                CP('dve', acum[i][:], psa[:, 0:16])
                TS('dve', nacum[i][:], psa[:, 0:16], -1.0)
                ACT(cdb[i][:], psa[:, 16:32], AF.Exp)
                TT('dve', dsd[i][:], psa[:, 16:32], acum[i][:], ALU.subtract)
                ACT(dsd[i][:], dsd[i][:], AF.Exp)
                ACT(ea[i][:], acum[i][:], AF.Exp)
                TT('dve', xdt[i][:].rearrange("p (h q) -> p h q", q=64), x_t[i][:].rearrange("p (h q) -> p h q", q=64),
                   bcl(dtv[:, c, :], [128, 16, 64]), ALU.mult)
                TT('dve', xsd[i][:].rearrange("p (h q) -> p h q", q=64), xdt[i][:].rearrange("p (h q) -> p h q", q=64),
                   bcl(dsd[i][:], [128, 16, 64]), ALU.mult)
                psc = bank()
                for g in range(2):
                    MM(psc[:, g * 128:(g + 1) * 128], lhsT=xcT[:, 8 + g, cs], rhs=xcT[:, 10 + g, cs], start=True, stop=True)
                CP('dve', CBm[i][:], psc[:, 0:256].rearrange("p (g l) -> p g l", g=2))
                TT('pool', diagA[:], bcm(ident_f, [128, 16, 128]), bcl(acum[i][:], [128, 16, 128]), ALU.mult)
                pse = [bank() for _ in range(4)]
                for q in range(4):
                    MM(pse[q][:, :], lhsT=ones_f[:, :], rhs=diagA[:, 4 * q:4 * q + 4, :], start=True, stop=False)
                    MM(pse[q][:, :], lhsT=ident_f, rhs=NEGm[:, :, :], start=False, stop=True)
                for h in range(16):
                    ACT(Em[i][:, h, :], pse[h // 4][:, (h % 4) * 128:(h % 4 + 1) * 128], AF.Exp, bias=nacum[i][:, h:h + 1])

            def ssd_back(c, i):
                cs = slice(c * 128, (c + 1) * 128)
                for g in range(2):
                    TT('dve', Em[i][:, 8 * g:8 * g + 8, :], Em[i][:, 8 * g:8 * g + 8, :],
                       CBm[i][:, g:g + 1, :].to_broadcast([128, 8, 128]), ALU.mult)
                psy = [bank(), bank()]
                for h in range(16):
                    reg = psy[h // 8][:, (h % 8) * 64:(h % 8 + 1) * 64]
                    MM(reg, lhsT=Em[i][:, h, :], rhs=xdt[i][:, h * 64:(h + 1) * 64], start=True, stop=False)
                    MM(reg, lhsT=Dident[:, h, :], rhs=x_t[i][:, h * 64:(h + 1) * 64], start=False, stop=True)
                pso = [bank(), bank()]
                for g in range(2):
                    MM(pso[g][:, :], lhsT=xcT[:, 10 + g, cs], rhs=HTb[:, g * 512:(g + 1) * 512], start=True, stop=True)
                for g in range(2):
                    TT('dve', yv[:, g * 512:(g + 1) * 512].rearrange("p (h q) -> p h q", q=64),
                       pso[g][:, :].rearrange("p (h q) -> p h q", q=64), bcl(ea[i][:, 8 * g:8 * g + 8], [128, 8, 64]), ALU.mult)
                    TT('dve', yv[:, g * 512:(g + 1) * 512], yv[:, g * 512:(g + 1) * 512], psy[g][:, :], ALU.add)
                pss2 = [bank(), bank()]
                for g in range(2):
                    MM(pss2[g][:, :], lhsT=B_t[i][:, g * 128:(g + 1) * 128], rhs=xsd[i][:, g * 512:(g + 1) * 512], start=True, stop=True)
                TT('dve', HT[:].rearrange("p (h q) -> p h q", q=64), HT[:].rearrange("p (h q) -> p h q", q=64),
                   bcl(cdb[i][:], [128, 16, 64]), ALU.mult)
                for g in range(2):
                    TT('dve', HT[:, g * 512:(g + 1) * 512], HT[:, g * 512:(g + 1) * 512], pss2[g][:, :], ALU.add)
                CP('act', HTb[:], HT[:])
                TT('dve', yv[:], yv[:], zs[:, c, :], ALU.mult)
                for g in range(2):
                    ACT(yv2[:, g * 512:(g + 1) * 512], yv[:, g * 512:(g + 1) * 512], AF.Square, accum_out=ssq4[:, g:g + 1])
                rstd_from(rs4[:, 0:2], ssq4[:, 0:2], 512.0, tmp4b[:, 0:2])
                for g in range(2):
                    TS('dve', ynb[:, g * 512:(g + 1) * 512], yv[:, g * 512:(g + 1) * 512], rs4[:, g:g + 1])
                pst3 = bank()
                pst3b = pst3[:, :].bitcast(BF16)
                for ft in range(8):
                    TR(pst3b[:, ft * 128:(ft + 1) * 128], ynb[:, ft * 128:(ft + 1) * 128], ident_b[:])
                TT('dve', ymixT[:, 8:16, cs], pst3b.rearrange("p (a b) -> p a b", b=128),
                   bcl(pcol[:, PC_SSDN:PC_SSDN + 8], [128, 8, 128]), ALU.mult)

            ssd_front(0, 0)
            for c in range(4):
                if c < 3:
                    ssd_front(c + 1, (c + 1) % 2)
                ssd_back(c, c % 2)
            if blk == 3:
                STo(np_ssdT_d.ap(), HT[:])
            if blk == 0:
                dbg('yssdT', ymixT[:, 8:16, :], [128, 8, 512], sc_)
            sc_.close()
            E.barrier(exclude='sp')
            if S3:
                E.pe_fast = 'smp' in FAST
                sc_ = ExitStack()
                p16 = sbx(sc_, "p16", [16, 1027])
                LD(p16[:], p16_d.ap())
                c0T = sbx(sc_, "c0T", [128, 12, 3, NS])
                acc = sbx(sc_, "acc", [128, 12, NS]); tcv = sbx(sc_, "tcv", [128, 12, NS]); xcs = sbx(sc_, "xcs", [128, 12, NS])
                dtls = sbx(sc_, "dtls", [16, NS]); decs = sbx(sc_, "decs", [16, NS]); Acol = sbx(sc_, "Acol", [16, 1])
                dtE = sbx(sc_, "dtE", [128, 8, NS]); decE = sbx(sc_, "decE", [128, 8, NS]); dtx = sbx(sc_, "dtx", [128, 8, NS])
                Dcol = sbx(sc_, "Dcol", [128, 8]); yvs = sbx(sc_, "yvs", [128, 8, NS]); y2s = sbx(sc_, "y2s", [128, 8, NS])
                sqs = sbx(sc_, "sqs", [128, 8, NS]); rgs = sbx(sc_, "rgs", [128, 2, NS]); rgt = sbx(sc_, "rgt", [128, 2, NS])
                dgb = [sbx(sc_, "dgb%d" % i, [128, 128]) for i in range(4)]
                h0s = [sbx(sc_, "h0s%d" % i, [128, 8, 128]) for i in range(2)]
                hns = [sbx(sc_, "hns%d" % i, [128, 8, 128]) for i in range(2)]
                prs = sbx(sc_, "prs", [128, 8, 128])
                for t_ in h0s + hns:
                    SPLIT[t_.name] = (1024, 128)
                LD(c0T[:].rearrange("p a b c -> p (a b c)"), conv0T_d.ap())
                E.dma(dq[0], ns_conv_a_d.ap().rearrange("b (k c) -> b k c", k=2), conv0_d.ap().rearrange("b (k c) -> b k c", k=3)[:, 1:3, :],
                      writes=[ns_conv_a_d.ap()])
                STo(ns_conv_bT_d.ap(), xbs[:].rearrange("p a b -> p (a b)"))
                TS('dve', dts[:], dts[:], p16[:, 0:1], None, ALU.add)
                TS('dve', dtls[:], dts[:], -1.0)
                TT('dve', dtls[:], dtls[:], dts[:], ALU.min)
                ACT(dtls[:], dtls[:], AF.Exp)
                TS('dve', dtls[:], dtls[:], 1.0, None, ALU.add)
                ACT(dtls[:], dtls[:], AF.Ln)
                STT(dts[:], dts[:], 0.0, dtls[:], ALU.max, ALU.add)
                ACT(Acol[:], p16[:, 1:2], AF.Exp)
                TS('dve', Acol[:], Acol[:], -1.0)
                TS('dve', dtls[:], dts[:], Acol[:, 0:1])
                ACT(decs[:], dtls[:], AF.Exp)
                cwv = pcol[:, PC_CW:PC_CW + 48].rearrange("p (k f) -> p k f", k=4)
                TT('dve', acc[:], c0T[:, :, 0, :], bcl(cwv[:, 0, :], [128, 12, NS]), ALU.mult)
                for k in (1, 2):
                    TT('dve', tcv[:], c0T[:, :, k, :], bcl(cwv[:, k, :], [128, 12, NS]), ALU.mult)
                    TT('dve', acc[:], acc[:], tcv[:], ALU.add)
                TT('dve', tcv[:], xbs[:], bcl(cwv[:, 3, :], [128, 12, NS]), ALU.mult)
                TT('dve', acc[:], acc[:], tcv[:], ALU.add)
                TT('dve', acc[:], acc[:], bcl(pcol[:, PC_CB:PC_CB + 12], [128, 12, NS]), ALU.add)
                ACT(xcs[:], acc[:], AF.Silu)
                pse = bank()
                for hp in range(8):
                    e16 = p16[:, 3 + hp * 128:3 + (hp + 1) * 128]
                    MM(pse[:, hp * NS:(hp + 1) * NS], lhsT=e16, rhs=dts[:], start=True, stop=True)
                    MM(pse[:, 128 + hp * NS:128 + (hp + 1) * NS], lhsT=e16, rhs=decs[:], start=True, stop=True)
                    MM(pse[:, 256 + hp:256 + hp + 1], lhsT=e16, rhs=p16[:, 2:3], start=True, stop=True)
                CP('dve', dtE[:], pse[:, 0:128].rearrange("p (a b) -> p a b", b=NS))
                CP('dve', decE[:], pse[:, 128:256].rearrange("p (a b) -> p a b", b=NS))
                CP('dve', Dcol[:], pse[:, 256:264])
                TT('dve', dtx[:], xcs[:, 0:8, :], dtE[:], ALU.mult)
                def bc_rows(b):
                    pb_ = bank()
                    for i in range(4):
                        TS('dve', dgb[i][:], ident_f, xcs[:, 8 + i, b:b + 1])
                        MM(pb_[:, i * 128:(i + 1) * 128], lhsT=ones_f[:, :], rhs=dgb[i][:], start=True, stop=True)
                    return pb_

                LD(h0s[0][:], ssd0_d.ap()[0].rearrange("(hp q) n -> q hp n", q=128))
                psb_next = bc_rows(0)
                for b in range(NS):
                    h0t = h0s[b % 2]; hnt = hns[b % 2]
                    psb = psb_next
                    if b + 1 < NS:
                        LD(h0s[(b + 1) % 2][:], ssd0_d.ap()[b + 1].rearrange("(hp q) n -> q hp n", q=128))
                        psb_next = bc_rows(b + 1)
                    for hp in range(8):
                        g = hp // 4
                        ACT(h0t[:, hp, :], h0t[:, hp, :], AF.Copy, scale=decE[:, hp, b:b + 1])
                        STT(hnt[:, hp, :], psb[:, g * 128:(g + 1) * 128], dtx[:, hp, b:b + 1], h0t[:, hp, :], ALU.mult, ALU.add)
                    STo(ns_ssd_d.ap()[b].rearrange("(hp q) n -> q hp n", q=128), hnt[:])
                    for g in range(2):
                        TT('dve', prs[:, 4 * g:4 * g + 4, :], hnt[:, 4 * g:4 * g + 4, :],
                           bcm(psb[:, (2 + g) * 128:(3 + g) * 128], [128, 4, 128]), ALU.mult)
                    RED(yvs[:, :, b], prs[:])
                TT('dve', y2s[:], xcs[:, 0:8, :], bcl(Dcol[:], [128, 8, NS]), ALU.mult)
                TT('dve', y2s[:], y2s[:], yvs[:], ALU.add)
                TT('dve', y2s[:], y2s[:], zT[:], ALU.mult)
                ACT(sqs[:], y2s[:], AF.Square)
                psg = bank()
                for g in range(2):
                    for i in range(4):
                        MM(psg[:, g * NS:(g + 1) * NS], lhsT=ones_f[:, :], rhs=sqs[:, 4 * g + i, :], start=(i == 0), stop=(i == 3))
                rstd_from(rgs[:].rearrange("p a b -> p (a b)"), psg[:, 0:2 * NS], 512.0, rgt[:].rearrange("p a b -> p (a b)"))
                for g in range(2):
                    TT('dve', y2s[:, 4 * g:4 * g + 4, :], y2s[:, 4 * g:4 * g + 4, :], rgs[:, g:g + 1, :].to_broadcast([128, 4, NS]), ALU.mult)
                TT('dve', ymixT_s[:, 8:16, :], y2s[:], bcl(pcol[:, PC_SSDN:PC_SSDN + 8], [128, 8, NS]), ALU.mult)
                sc_.close()
                E.barrier(exclude='sp')
            ss_small.close()
        if blk < 3:
            load_hnT(xp_T, (blk + 1) * 512, 512, hnT3, rstdbc_all[:, (blk + 1) * 512:(blk + 2) * 512])
        if blk == 2:
            load_hnT(xs_T, 0, NS, hnT3_s, rstdbc_s[:, :])
        E.pe_fast = 'dense' in FAST
        sxs = ExitStack()
        TL = [dict(x=x_tm[:, tt, :], rows=128, ym=ymixT, hf=hfT3, c0=tt * 128, smp=False, tt=tt) for tt in range(4)]
        if S3:
            x_tm_s = sbx(sxs, "x_tm_s", [NS, 1024])
            LD(x_tm_s[:], xs_tm.ap())
            TL.append(dict(x=x_tm_s[:, :], rows=NS, ym=ymixT_s, hf=hfT3_s, c0=0, smp=True, tt=0))
        for half in range(2):
            pss_ = [bank() for _ in TL]
            for kh in range(2):
                wo = wunit(ring, rgi, w_out_d, kh * 1024, half * 512, bf=wb_out)
                for ti, tl in enumerate(TL):
                    r = tl['rows']
                    for kt in range(8):
                        MM(pss_[ti][0:r, :], lhsT=tl['ym'][:, kh * 8 + kt, tl['c0']:tl['c0'] + r], rhs=wo[:, kt, :],
                           start=(kh == 0 and kt == 0), stop=(kh == 1 and kt == 7))
            for ti, tl in enumerate(TL):
                r = tl['rows']
                xs_ = tl['x'][0:r, half * 512:(half + 1) * 512]
                TT('dve', xs_, xs_, pss_[ti][0:r, :], ALU.add)
        if blk == 0:
            dbg('x1', x_tm[:], [128, 4, 1024], None, F32)
        with ExitStack() as se:
            hfb = sbx(se, "hfb", [128, 4, 1024], BF16)
            hidT = sbx(se, "hidT", [128, 32, 512], BF16)
            SPLIT[hfb.name] = (4096, 1024); SPLIT[hidT.name] = (32 * 512, 512)
            rl = [sbx(se, "rl%d" % i, [128, 512], BF16) for i in range(2)]
            junk3 = sbx(se, "junk3", [128, 1024])
            if S3:
                hfb_s = sbx(se, "hfb_s", [NS, 1024], BF16); hidT_s = sbx(se, "hidT_s", [128, 32, NS], BF16)
                rl_s = [sbx(se, "rl_s%d" % i, [128, NS], BF16) for i in range(2)]

            def norm_stats(tl):
                r = tl['rows']
                if tl['smp']:
                    sq_, rs_, tm_ = ssq4s[:, 0:1], rs4s[:, 0:1], tmp4s[:, 0:1]
                else:
                    tt = tl['tt']
                    sq_, rs_, tm_ = ssq4[:, tt:tt + 1], rs4[:, tt:tt + 1], tmp4b[:, tt:tt + 1]
                ACT(junk3[0:r, :], tl['x'][0:r, :], AF.Square, accum_out=sq_)
                rstd_from(rs_, sq_, 1024.0, tm_)
                return rs_

            for tl in TL:
                r = tl['rows']
                rs_ = norm_stats(tl)
                hb = hfb_s[:, :] if tl['smp'] else hfb[:, tl['tt'], :]
                TS('dve', hb[0:r, :], tl['x'][0:r, :], rs_)
                pst = bank()
                pstb = pst[:, :].bitcast(BF16)
                for kt in range(8):
                    TR(pstb[:, kt * 128:kt * 128 + r], hb[0:r, kt * 128:(kt + 1) * 128], ident_b[0:r, 0:r])
                TT('dve', tl['hf'][:, :, tl['c0']:tl['c0'] + r], pstb.rearrange("p (a b) -> p a b", b=128)[:, :, 0:r],
                   bcl(pcol[:, PC_FFN:PC_FFN + 8], [128, 8, r]), ALU.mult)
            for un in range(8):
                w1 = wunit(ring, rgi, w_ff1_d, 0, 512 * un, bf=wb_ff1)
                for sub in range(4):
                    ft = 4 * un + sub
                    ps = bank()
                    for kt in range(8):
                        MM(ps[:, :], lhsT=w1[:, kt, sub * 128:(sub + 1) * 128], rhs=hfT3[:, kt, :], start=(kt == 0), stop=(kt == 7))
                    ACT(rl[ft % 2][:, :], ps[:, :], AF.Relu)
                    TT('dve', hidT[:, ft, :], rl[ft % 2][:, :], rl[ft % 2][:, :], ALU.mult)
                    if S3:
                        ps = bank()
                        for kt in range(8):
                            MM(ps[:, 0:NS], lhsT=w1[:, kt, sub * 128:(sub + 1) * 128], rhs=hfT3_s[:, kt, :], start=(kt == 0), stop=(kt == 7))
                        ACT(rl_s[ft % 2][:, :], ps[:, 0:NS], AF.Relu)
                        TT('dve', hidT_s[:, ft, :], rl_s[ft % 2][:, :], rl_s[ft % 2][:, :], ALU.mult)
            for half in range(2):
                pss_ = [bank() for _ in TL]
                for q in range(4):
                    w2 = wunit(ring, rgi, w_ff2_d, q * 1024, half * 512, bf=wb_ff2)
                    for ti, tl in enumerate(TL):
                        r = tl['rows']
                        hd = hidT_s if tl['smp'] else hidT
                        for kt in range(8):
                            MM(pss_[ti][0:r, :], lhsT=hd[:, q * 8 + kt, tl['c0']:tl['c0'] + r], rhs=w2[:, kt, :],
                               start=(q == 0 and kt == 0), stop=(q == 3 and kt == 7))
                for ti, tl in enumerate(TL):
                    r = tl['rows']
                    xs_ = tl['x'][0:r, half * 512:(half + 1) * 512]
                    TT('dve', xs_, xs_, pss_[ti][0:r, :], ALU.add)
            E.pe_fast = False
            for tl in TL:
                r = tl['rows']
                rs_ = norm_stats(tl)
                STT(tl['x'][0:r, :], tl['x'][0:r, :], rs_, pbc[0:r, PB_FIN:PB_FIN + 1024], ALU.mult, ALU.mult)
            STo(y_p_d.ap()[blk * 512:(blk + 1) * 512, :].rearrange("(tt p) f -> p tt f", p=128), x_tm[:])
            if S3:
                STo(y_s_d.ap(), x_tm_s[:, :])
        E.barrier(exclude='sp')
        sxs.close()

    E.finish('sp')
    E.build()
    st.close()
    return nc, dbg_d, E


def _l1(a):
    sh = a.shape[2:]
    a = a.reshape((32, 2, 64) + sh)
    perm = (1, 2, 0) + tuple(range(3, 3 + len(sh)))
    return np.ascontiguousarray(a.transpose(perm)).reshape((128, 32) + sh)


def _host_inputs(inp, c):
    f = np.float32
    g = lambda k: np.asarray(inp[k], dtype=f)
    xp = g("x_prompt")[c]
    xs = g("x_sample")[16 * c:16 * c + 16, 0]
    m = {}
    m["xp_tm"] = np.ascontiguousarray(xp); m["xp_T"] = np.ascontiguousarray(xp.T)
    m["xs_tm"] = np.ascontiguousarray(xs); m["xs_T"] = np.ascontiguousarray(xs.T)
    col = lambda v: np.ascontiguousarray(v.reshape(-1, 128).T)
    cw = g("ssd_conv_w")[0]
    pcol = np.concatenate([col(g("norm_mix_w")[0]), col(g("s5_norm_w")[0]), col(g("s5_b_glu")[0]), col(g("s5_d")[0]),
                           col(g("ssd_norm_w")[0]),
                           np.ascontiguousarray(cw.reshape(4, 12, 128).transpose(2, 0, 1)).reshape(128, 48),
                           col(g("ssd_conv_b")[0]), col(g("norm_ffn_w")[0]),
                           np.broadcast_to(np.arange(9, dtype=f).reshape(1, 9), (128, 9))], axis=1)
    m["pcol"] = np.ascontiguousarray(pcol)
    bc = lambda v: np.broadcast_to(v.reshape(1, -1), (128, v.size))
    m["pbc"] = np.ascontiguousarray(np.concatenate([bc(g("norm_final_w")),
                                                    bc(g("ssd_dt_bias")[0]), bc(g("ssd_a_log")[0]), bc(g("ssd_d")[0])], axis=1))
    e16 = np.zeros((16, 1024), f)
    for h in range(16):
        e16[h, h * 64:(h + 1) * 64] = 1.0
    m["p16"] = np.ascontiguousarray(np.concatenate([g("ssd_dt_bias")[0].reshape(16, 1), g("ssd_a_log")[0].reshape(16, 1),
                                                    g("ssd_d")[0].reshape(16, 1), e16], axis=1))
    tri = np.triu(np.ones((128, 128), f))
    m["cst"] = np.ascontiguousarray(np.concatenate([np.eye(128, dtype=f), tri], axis=1))
    lre, lim, ldt = g("s5_lam_re")[0], g("s5_lam_im")[0], g("s5_log_dt")[0]
    bre, bim, cre, cim = g("s5_b_re")[0], g("s5_b_im")[0], g("s5_c_re")[0], g("s5_c_im")[0]

    def cz(cc):
        t = _l1(np.ascontiguousarray(cc.transpose(0, 2, 1)))
        out = np.zeros((128, 32, 2, 16), f)
        out[0:64, :, 0, :] = t[0:64]
        out[64:128, :, 1, :] = t[64:128]
        return out.reshape(128, 1024)

    def bz(bb):
        t = _l1(bb)
        out = np.zeros((128, 32, 2, 16), f)
        out[0:64, :, 0, :] = t[0:64]
        out[64:128, :, 1, :] = t[64:128]
        return out.reshape(128, 1024)

    m["l1pack"] = np.ascontiguousarray(np.concatenate([_l1(lre), _l1(lim), _l1(np.repeat(ldt[:, None], 64, 1)), cz(cre), cz(cim), bz(bre), bz(bim)], axis=1))
    m["gpack"] = np.ascontiguousarray(np.concatenate([lre, lim, ldt.reshape(64, 1)], axis=1))

    def b2(bb):
        out = np.zeros((8, 16, 8, 2, 64), f)
        t = bb.reshape(8, 8, 64, 16)
        for g8 in range(8):
            out[g8, :, :, g8 % 2, :] = t[:, g8].transpose(2, 0, 1)
        return out.reshape(128, 1024)

    m["b2pack"] = np.ascontiguousarray(np.concatenate([b2(bre), b2(bim)], axis=1))
    h0r = g("state_s5_re")[0, 16 * c:16 * c + 16]
    h0i = g("state_s5_im")[0, 16 * c:16 * c + 16]
    h0 = np.stack([_l1(np.ascontiguousarray(h0r.transpose(1, 2, 0))), _l1(np.ascontiguousarray(h0i.transpose(1, 2, 0)))], axis=1)
    m["h0l1"] = np.ascontiguousarray(h0.reshape(128, 1024))
    m["ssd0"] = np.ascontiguousarray(g("state_ssd")[0, 16 * c:16 * c + 16].reshape(16, 1024, 128))
    cv = g("state_conv")[0, 16 * c:16 * c + 16]
    m["conv0"] = np.ascontiguousarray(cv.reshape(16, 3 * 1536))
    m["conv0T"] = np.ascontiguousarray(cv.reshape(16, 3, 12, 128).transpose(3, 2, 1, 0)).reshape(128, 12 * 3 * 16)
    m["w_in"] = g("w_in")[0]; m["w_glu"] = g("s5_w_glu")[0]; m["w_out"] = g("w_out")[0]
    m["w_ff1"] = g("w_ff1")[0]; m["w_ff2"] = g("w_ff2")[0]
    return m


def _unl1(a):
    sh = a.shape[2:]
    a = a.reshape((2, 64, 32) + sh)
    perm = (2, 0, 1) + tuple(range(3, 3 + len(sh)))
    return np.ascontiguousarray(a.transpose(perm)).reshape((64, 64) + sh)


_CACHE = {}


def kernel(**inputs):
    if "nc" not in _CACHE:
        _CACHE["nc"] = build_program()
    nc, _, _ = _CACHE["nc"]
    shared = None
    in_maps = []
    for c in range(8):
        in_maps.append(_host_inputs(inputs, c))
    res = run_bass_kernel_spmd(nc, in_maps, core_ids=list(range(8)))
    R = res.results
    f = np.float32
    y_p = np.stack([R[c]["y_p"] for c in range(8)]).astype(f)
    y_s = np.concatenate([R[c]["y_s"] for c in range(8)]).reshape(128, 1, 1024).astype(f)
    np_re = np.stack([_unl1(R[c]["np5"].reshape(128, 2, 32)[:, 0, :]) for c in range(8)])[None].astype(f)
    np_im = np.stack([_unl1(R[c]["np5"].reshape(128, 2, 32)[:, 1, :]) for c in range(8)])[None].astype(f)
    np_ssd = np.stack([R[c]["np_ssdT"].T.reshape(16, 64, 128) for c in range(8)])[None].astype(f)
    np_conv = np.stack([R[c]["np_convT"].reshape(128, 12, 3).transpose(2, 1, 0).reshape(3, 1536) for c in range(8)])[None].astype(f)
    ns_re = np.concatenate([_unl1(R[c]["ns5"].reshape(128, 2, 32, 16)[:, 0]).transpose(2, 0, 1) for c in range(8)])[None].astype(f)
    ns_im = np.concatenate([_unl1(R[c]["ns5"].reshape(128, 2, 32, 16)[:, 1]).transpose(2, 0, 1) for c in range(8)])[None].astype(f)
    ns_ssd = np.concatenate([R[c]["ns_ssd"].reshape(16, 16, 64, 128) for c in range(8)])[None].astype(f)
    ns_conv = np.concatenate([np.concatenate([R[c]["ns_conv_a"].reshape(16, 2, 1536),
                                              R[c]["ns_conv_bT"].reshape(128, 12, 16).transpose(2, 1, 0).reshape(16, 1, 1536)], axis=1)
                              for c in range(8)])[None].astype(f)
    return (np.ascontiguousarray(y_p), np.ascontiguousarray(y_s), np.ascontiguousarray(np_re), np.ascontiguousarray(np_im),
            np.ascontiguousarray(np_ssd), np.ascontiguousarray(np_conv), np.ascontiguousarray(ns_re), np.ascontiguousarray(ns_im),
            np.ascontiguousarray(ns_ssd), np.ascontiguousarray(ns_conv))
```
